# Optimizing a Trainium2 kernel written in Bass

```python
import math
import jax, jax.numpy as jnp
from jax import lax
import numpy as np

D_MODEL = 1024
BATCH = 8
SEQ = 4096
DEPTH = 2

MIX_W = D_MODEL
SSD_HEAD_DIM = 64
SSD_INNER = MIX_W // 2
SSD_HEADS = SSD_INNER // SSD_HEAD_DIM
SSD_GROUPS = 2
SSD_HPG = SSD_HEADS // SSD_GROUPS
SSD_STATE = 128
SSD_CONV_K = 4
SSD_CHUNK = 256
SSD_CONV_CH = SSD_INNER + 2 * SSD_GROUPS * SSD_STATE
SSD_PROJ_W = SSD_INNER + SSD_CONV_CH + SSD_HEADS
POOL_W = MIX_W // 4
POOL_WINDOWS = (2, 4, 8, 16)
POOL_GROUPS = len(POOL_WINDOWS)
POOL_CH = POOL_W // POOL_GROUPS
ATT_HEAD_DIM = 64
ATT_W = MIX_W - SSD_INNER - POOL_W
ATT_HEADS = ATT_W // ATT_HEAD_DIM
ATT_PATTERNS = ((128, 1), (512, 4), (2048, 16))
ATT_BLOCK = 128
ROT_DIM = ATT_HEAD_DIM // 4
ROPE_THETA = 500000.0
IN_W = SSD_PROJ_W + POOL_W + 3 * ATT_W
FFN_DIM = 2816
FFN_CONV_K = 3
NORM_EPS = 1e-6

kernel_name = "hybrid_ssd_pool_dilated_attn_trunk"

F32 = jnp.float32


def rmsnorm(x, g):
    xf = x.astype(F32)
    y = xf * lax.rsqrt(jnp.mean(xf * xf, axis=-1, keepdims=True) + NORM_EPS)
    return (y * g.astype(F32)).astype(x.dtype)


def causal_dwconv(x, w, b):
    k, ch = w.shape
    y = lax.conv_general_dilated(x, w.astype(x.dtype)[:, None, :], window_strides=(1,),
                                 padding=[(k - 1, 0)], dimension_numbers=("NWC", "WIO", "NWC"),
                                 feature_group_count=ch)
    return y + b.astype(x.dtype)


def ssd_chunked(xs, da, bm, cm):
    b, s, g, j, p = xs.shape
    n = bm.shape[-1]
    pad = (-s) % SSD_CHUNK
    sp = s + pad
    nc = sp // SSD_CHUNK
    q = SSD_CHUNK
    xs = jnp.pad(xs, ((0, 0), (0, pad), (0, 0), (0, 0), (0, 0))).reshape(b, nc, q, g, j, p)
    da = jnp.pad(da, ((0, 0), (0, pad), (0, 0), (0, 0))).reshape(b, nc, q, g, j)
    bm = jnp.pad(bm, ((0, 0), (0, pad), (0, 0), (0, 0))).reshape(b, nc, q, g, n)
    cm = jnp.pad(cm, ((0, 0), (0, pad), (0, 0), (0, 0))).reshape(b, nc, q, g, n)
    a_cum = jnp.cumsum(da, axis=2)
    acs = jnp.moveaxis(a_cum, 2, -1)
    seg = acs[..., :, None] - acs[..., None, :]
    causal = jnp.tril(jnp.ones((q, q), dtype=bool))
    lmat = jnp.exp(jnp.where(causal, seg, -jnp.inf))
    cb = jnp.einsum("bclgn,bcsgn->bcgls", cm, bm)
    y_diag = jnp.einsum("bcgjls,bcsgjp->bclgjp", cb[:, :, :, None] * lmat, xs)
    decay_states = jnp.exp(a_cum[:, :, -1:] - a_cum)
    states = jnp.einsum("bclgn,bclgj,bclgjp->bcgjpn", bm, decay_states, xs)
    chunk_decay = jnp.exp(a_cum[:, :, -1])

    def step(h, inp):
        st, dec = inp
        return h * dec[..., None, None] + st, h

    h0 = jnp.zeros((b, g, j, p, n), xs.dtype)
    _, h_in = lax.scan(step, h0, (jnp.moveaxis(states, 1, 0), jnp.moveaxis(chunk_decay, 1, 0)))
    h_in = jnp.moveaxis(h_in, 0, 1)
    y_off = jnp.einsum("bclgn,bcgjpn,bclgj->bclgjp", cm, h_in, jnp.exp(a_cum))
    return (y_diag + y_off).reshape(b, sp, g, j, p)[:, :s]


def ssd_mixer(p_in, conv_w, conv_b, dt_bias, a_log, d_skip, norm_g):
    b, s, _ = p_in.shape
    z = p_in[..., :SSD_INNER].astype(F32)
    xbc = p_in[..., SSD_INNER:SSD_INNER + SSD_CONV_CH]
    dt = p_in[..., SSD_INNER + SSD_CONV_CH:].astype(F32)
    xbc = jax.nn.silu(causal_dwconv(xbc, conv_w, conv_b)).astype(F32)
    xs = xbc[..., :SSD_INNER].reshape(b, s, SSD_GROUPS, SSD_HPG, SSD_HEAD_DIM)
    bm = xbc[..., SSD_INNER:SSD_INNER + SSD_GROUPS * SSD_STATE].reshape(b, s, SSD_GROUPS, SSD_STATE)
    cm = xbc[..., SSD_INNER + SSD_GROUPS * SSD_STATE:].reshape(b, s, SSD_GROUPS, SSD_STATE)
    dt = jax.nn.softplus(dt + dt_bias.astype(F32)).reshape(b, s, SSD_GROUPS, SSD_HPG)
    a = -jnp.exp(a_log.astype(F32)).reshape(SSD_GROUPS, SSD_HPG)
    y = ssd_chunked(xs * dt[..., None], dt * a, bm, cm)
    y = y + d_skip.astype(F32).reshape(SSD_GROUPS, SSD_HPG, 1) * xs
    y = y.reshape(b, s, SSD_GROUPS, SSD_HPG * SSD_HEAD_DIM) * jax.nn.silu(z).reshape(b, s, SSD_GROUPS, -1)
    y = y * lax.rsqrt(jnp.mean(y * y, axis=-1, keepdims=True) + NORM_EPS)
    y = y * norm_g.astype(F32).reshape(SSD_GROUPS, -1)
    return y.reshape(b, s, SSD_INNER)


def pool_mixer(u, pool_w, pool_scale):
    b, s, _ = u.shape
    u = u.astype(F32).reshape(b, s, POOL_GROUPS, POOL_CH)
    cs = jnp.cumsum(u, axis=1)
    outs = []
    for gi, w in enumerate(POOL_WINDOWS):
        cg = cs[:, :, gi]
        lag = jnp.pad(cg, ((0, 0), (w, 0), (0, 0)))[:, :s]
        cnt = jnp.minimum(jnp.arange(1, s + 1), w).astype(F32)[None, :, None]
        outs.append((cg - lag) / cnt)
    pooled = jnp.stack(outs, axis=2)
    y = jnp.einsum("bsgc,gcd->bsgd", pooled - u, pool_w.astype(F32))
    return y.reshape(b, s, POOL_W) * pool_scale.astype(F32)


def partial_rope(t, cos, sin):
    half = ROT_DIM // 2
    t1, t2 = t[..., :half], t[..., half:ROT_DIM]
    c, s = cos[:, :, None], sin[:, :, None]
    return jnp.concatenate([t1 * c - t2 * s, t2 * c + t1 * s, t[..., ROT_DIM:]], axis=-1)


def dilated_branch(q, k, v, window, dilation):
    b, s, h, e = q.shape
    L = s // dilation
    steps = window // dilation
    n_prev = -(-steps // ATT_BLOCK)
    nb = -(-L // ATT_BLOCK)
    lp = nb * ATT_BLOCK

    def strided(t):
        t = t.reshape(b, L, dilation, h, e).transpose(0, 2, 3, 1, 4)
        t = jnp.pad(t, ((0, 0), (0, 0), (0, 0), (0, lp - L), (0, 0)))
        return t.reshape(b, dilation, h, nb, ATT_BLOCK, e)

    def band(t):
        tp = jnp.pad(t, ((0, 0), (0, 0), (0, 0), (n_prev, 0), (0, 0), (0, 0)))
        return jnp.concatenate([tp[:, :, :, j:j + nb] for j in range(n_prev + 1)], axis=4)

    qb, kb, vb = strided(q), strided(k), strided(v)
    kband, vband = band(kb), band(vb)
    sc = jnp.einsum("bdhnqe,bdhnke->bdhnqk", qb, kband) * (e ** -0.5)
    qi = jnp.arange(ATT_BLOCK)[:, None] + n_prev * ATT_BLOCK
    kj = jnp.arange((n_prev + 1) * ATT_BLOCK)[None, :]
    rel = qi - kj
    kpos = jnp.arange(nb)[:, None, None] * ATT_BLOCK + kj[None] - n_prev * ATT_BLOCK
    valid = (rel >= 0) & (rel <= steps) & (kpos >= 0)
    sc = jnp.where(valid, sc, -jnp.inf)
    m = jnp.max(sc, axis=-1, keepdims=True)
    p = jnp.exp(sc - m)
    l = jnp.sum(p, axis=-1)
    o = jnp.einsum("bdhnqk,bdhnke->bdhnqe", p, vband) / l[..., None]
    lse = m[..., 0] + jnp.log(l)
    o = o.reshape(b, dilation, h, lp, e)[:, :, :, :L].transpose(0, 3, 1, 2, 4).reshape(b, s, h, e)
    lse = lse.reshape(b, dilation, h, lp)[..., :L].transpose(0, 3, 1, 2).reshape(b, s, h)
    return o, lse


def dilated_attention(qkv, cos, sin):
    b, s, _ = qkv.shape
    qkv = qkv.astype(F32)
    q = partial_rope(qkv[..., :ATT_W].reshape(b, s, ATT_HEADS, ATT_HEAD_DIM), cos, sin)
    k = partial_rope(qkv[..., ATT_W:2 * ATT_W].reshape(b, s, ATT_HEADS, ATT_HEAD_DIM), cos, sin)
    v = qkv[..., 2 * ATT_W:].reshape(b, s, ATT_HEADS, ATT_HEAD_DIM)
    outs, lses = [], []
    for window, dilation in ATT_PATTERNS:
        o, lse = dilated_branch(q, k, v, window, dilation)
        outs.append(o)
        lses.append(lse)
    wts = jax.nn.softmax(jnp.stack(lses, axis=0), axis=0)
    o = jnp.einsum("rbsh,rbshe->bshe", wts, jnp.stack(outs, axis=0))
    return o.reshape(b, s, ATT_W)


def conv_ffn(h, up, conv_w, conv_b, down):
    hid = causal_dwconv(h @ up, conv_w, conv_b)
    g, u = jnp.split(hid, 2, axis=-1)
    return (jax.nn.silu(g) * u) @ down


def setup_inputs(seed: int = 0) -> dict:
    key = jax.random.key(seed)
    ks = jax.random.split(key, 24)
    nrm = lambda k, shape, scale: jax.random.normal(k, shape, F32) * scale
    dt = jnp.exp(jax.random.uniform(ks[9], (DEPTH, SSD_HEADS), F32) * (math.log(0.1) - math.log(0.001))
                 + math.log(0.001))
    return {
        "x": nrm(ks[0], (BATCH, SEQ, D_MODEL), 1.0),
        "c": nrm(ks[1], (BATCH, D_MODEL), 1.0),
        "positions": (jax.random.randint(ks[2], (BATCH, 1), 0, 1024, jnp.int32)
                      + jnp.arange(SEQ, dtype=jnp.int32)[None, :]),
        "ada_w": nrm(ks[3], (DEPTH, D_MODEL, 6 * D_MODEL), 0.5 * D_MODEL ** -0.5),
        "ada_b": nrm(ks[4], (DEPTH, 6 * D_MODEL), 0.02),
        "norm1_g": 1.0 + nrm(ks[5], (DEPTH, D_MODEL), 0.02),
        "w_in": nrm(ks[6], (DEPTH, D_MODEL, IN_W), D_MODEL ** -0.5),
        "ssd_conv_w": nrm(ks[7], (DEPTH, SSD_CONV_K, SSD_CONV_CH), SSD_CONV_K ** -0.5),
        "ssd_conv_b": nrm(ks[8], (DEPTH, SSD_CONV_CH), 0.02),
        "ssd_dt_bias": dt + jnp.log(-jnp.expm1(-dt)),
        "ssd_a_log": jnp.log(jax.random.uniform(ks[10], (DEPTH, SSD_HEADS), F32, 1.0, 16.0)),
        "ssd_d": 1.0 + nrm(ks[11], (DEPTH, SSD_HEADS), 0.02),
        "ssd_norm_g": 1.0 + nrm(ks[12], (DEPTH, SSD_INNER), 0.02),
        "pool_w": nrm(ks[13], (DEPTH, POOL_GROUPS, POOL_CH, POOL_CH), POOL_CH ** -0.5),
        "pool_scale": 1.0 + nrm(ks[14], (DEPTH, POOL_W), 0.02),
        "w_out": nrm(ks[15], (DEPTH, MIX_W, D_MODEL), MIX_W ** -0.5),
        "norm2_g": 1.0 + nrm(ks[16], (DEPTH, D_MODEL), 0.02),
        "ffn_up": nrm(ks[17], (DEPTH, D_MODEL, 2 * FFN_DIM), D_MODEL ** -0.5),
        "ffn_conv_w": nrm(ks[18], (DEPTH, FFN_CONV_K, 2 * FFN_DIM), FFN_CONV_K ** -0.5),
        "ffn_conv_b": nrm(ks[19], (DEPTH, 2 * FFN_DIM), 0.02),
        "ffn_down": nrm(ks[20], (DEPTH, FFN_DIM, D_MODEL), FFN_DIM ** -0.5),
        "final_g": 1.0 + nrm(ks[21], (D_MODEL,), 0.02),
    }


def reference(x, c, positions, ada_w, ada_b, norm1_g, w_in, ssd_conv_w, ssd_conv_b, ssd_dt_bias,
              ssd_a_log, ssd_d, ssd_norm_g, pool_w, pool_scale, w_out, norm2_g, ffn_up, ffn_conv_w,
              ffn_conv_b, ffn_down, final_g):
    inv_freq = ROPE_THETA ** (-jnp.arange(0, ROT_DIM, 2, dtype=F32) / ROT_DIM)
    ang = positions.astype(F32)[..., None] * inv_freq
    cos, sin = jnp.cos(ang), jnp.sin(ang)
    c_act = jax.nn.silu(c)
    a0 = SSD_PROJ_W
    a1 = SSD_PROJ_W + POOL_W
    for i in range(DEPTH):
        mod = (c_act @ ada_w[i] + ada_b[i])[:, None, :]
        sh1, sc1, g1, sh2, sc2, g2 = jnp.split(mod, 6, axis=-1)
        h = rmsnorm(x, norm1_g[i]) * (1.0 + sc1) + sh1
        proj = h @ w_in[i]
        y_ssd = ssd_mixer(proj[..., :a0], ssd_conv_w[i], ssd_conv_b[i], ssd_dt_bias[i],
                          ssd_a_log[i], ssd_d[i], ssd_norm_g[i])
        y_pool = pool_mixer(proj[..., a0:a1], pool_w[i], pool_scale[i])
        y_att = dilated_attention(proj[..., a1:], cos, sin)
        mix = jnp.concatenate([y_ssd, y_pool, y_att], axis=-1).astype(x.dtype)
        x = x + g2.dtype.type(1) * g1 * (mix @ w_out[i]) if False else x + g1 * (mix @ w_out[i])
        h = rmsnorm(x, norm2_g[i]) * (1.0 + sc2) + sh2
        x = x + g2 * conv_ffn(h, ffn_up[i], ffn_conv_w[i], ffn_conv_b[i], ffn_down[i])
    return rmsnorm(x, final_g)
```

```python
import numpy as np
from contextlib import ExitStack
import concourse.bass as bass
import concourse.mybir as mybir
from concourse.bass_utils import run_bass_kernel_spmd

F32 = mybir.dt.float32
BF16 = mybir.dt.bfloat16
I32 = mybir.dt.int32
ALU = mybir.AluOpType
AF = mybir.ActivationFunctionType
AX = mybir.AxisListType

S = 4096
D = 1024
DEPTH = 2
NT = 8
TT = 512
FFN = 2816
NFC = 22
EPS = 1e-6
IN_W = 2568
C_Z, C_XBC, C_DT, C_U, C_Q, C_K, C_V = 0, 512, 1536, 1544, 1800, 2056, 2312

SAME_SYNC = True
ENGS = ("pe", "act", "dve", "pool", "sp")


class View:
    __slots__ = ("tile", "ap")

    def __init__(self, tile, ap):
        self.tile = tile
        self.ap = ap

    def map(self, f):
        return View(self.tile, f(self.ap))


class Tile:
    def __init__(self, t):
        self.t = t
        self.w = {}
        self.r = {}

    def __getitem__(self, k):
        return View(self, self.t[k])


def _tiles(vs):
    out = []
    for v in vs:
        if isinstance(v, View):
            out.append(v.tile)
        elif isinstance(v, Tile):
            out.append(v)
    return out


def _ap(v):
    return v.ap if isinstance(v, View) else v


class Prog:
    NS = 8

    def __init__(self):
        self.nc = bass.Bass("TRN2", target_bir_lowering=False)
        self.es = ExitStack()
        nc = self.nc
        self.q = {e: [] for e in ENGS}
        self.cnt = {e: 0 for e in ENGS}
        self.sem = {e: self.es.enter_context(nc.semaphore("s_" + e)) for e in ENGS}
        self.dsem = {e: [self.es.enter_context(nc.semaphore("d_%s%d" % (e, i))) for i in range(self.NS)]
                     for e in ("sp", "pool")}
        self.dcnt = {"sp": 0, "pool": 0}
        self.waited = {e: {} for e in ENGS}
        self.ninst = 0
        self.dbg = set()

    def sb(self, name, shape, dt):
        return Tile(self.es.enter_context(self.nc.sbuf_tensor("sb_" + name, list(shape), dt)))

    def ps(self, name, shape, dt=F32):
        return Tile(self.es.enter_context(self.nc.psum_tensor(name, list(shape), dt)))

    def dram(self, name, shape, dt, kind="Internal"):
        if name in self.dbg:
            kind = "ExternalOutput"
        return Tile(self.nc.dram_tensor(name, list(shape), dt, kind=kind).ap())

    def _deps(self, e, reads, writes):
        need = {}

        def add(d):
            for s, (sem, v) in d.items():
                if s not in need or need[s][1] < v:
                    need[s] = (sem, v)

        for t in reads:
            add(t.w)
        for t in writes:
            add(t.w)
            add(t.r)
        own = self.sem[e].num
        for s, (sem, v) in need.items():
            if s == own and (e == "pe" or not SAME_SYNC):
                continue
            if self.waited[e].get(s, 0) >= v:
                continue
            self.waited[e][s] = v
            self.q[e].append(("w", sem, v))

    def _mark(self, tok, reads, writes):
        sem, v = tok
        for t in reads:
            if t.r.get(sem.num, (None, 0))[1] < v:
                t.r[sem.num] = tok
        for t in writes:
            t.r = {}
            if t.w.get(sem.num, (None, 0))[1] < v:
                t.w[sem.num] = tok

    def emit(self, e, fn, reads=(), writes=(), sig=True):
        reads = _tiles(reads)
        writes = _tiles(writes)
        self._deps(e, reads, writes)
        sem = self.sem[e]
        if sig:
            self.cnt[e] += 1
            val = self.cnt[e]
        else:
            val = self.cnt[e] + 1
        self.q[e].append(("i", fn, sem if sig else None, 1))
        self._mark((sem, val), reads, writes)
        self.ninst += 1

    def dma(self, out, in_, e="sp", **kw):
        j = self.dcnt[e]
        self.dcnt[e] += 1
        dsem = self.dsem[e][j % self.NS]
        if j >= self.NS:
            v = 16 * (j // self.NS)
            if self.waited[e].get(dsem.num, 0) < v:
                self.waited[e][dsem.num] = v
                self.q[e].append(("w", dsem, v))
        reads = _tiles([in_])
        writes = _tiles([out])
        self._deps(e, reads, writes)
        oa, ia = out.ap, in_.ap
        self.q[e].append(("i", lambda eng: eng.dma_start(out=oa, in_=ia, **kw), dsem, 16))
        self._mark((dsem, 16 * (j // self.NS + 1)), reads, writes)
        self.ninst += 1

    def wait_all_writes(self, e, tiles):
        self._deps(e, _tiles(tiles), [])

    def eng(self, e):
        return e

    def mm(self, out, lhsT, rhs, start=True, stop=True, sig=True):
        oa, la, ra = out.ap, lhsT.ap, rhs.ap
        self.emit("pe", lambda t: t.matmul(oa, la, ra, start=start, stop=stop), [lhsT, rhs], [out], sig)

    def transpose(self, out, in_, ident, sig=True):
        oa, ia, da = out.ap, in_.ap, ident.ap
        self.emit("pe", lambda t: t.transpose(oa, ia, da), [in_, ident], [out], sig)

    def act(self, out, in_, func, bias=0.0, scale=1.0, e="act"):
        oa, ia, ba, sa = out.ap, in_.ap, _ap(bias), _ap(scale)
        self.emit("act", lambda a: a.activation(oa, ia, func, bias=ba, scale=sa), [in_, bias, scale], [out])

    def tt(self, e, out, in0, in1, op):
        oa, a0, a1 = out.ap, in0.ap, in1.ap
        self.emit(e, lambda v: v.tensor_tensor(oa, a0, a1, op), [in0, in1], [out])

    def ts(self, e, out, in0, s1, op0, s2=None, op1=ALU.bypass):
        oa, a0, b1, b2 = out.ap, in0.ap, _ap(s1), _ap(s2)
        self.emit(e, lambda v: v.tensor_scalar(oa, a0, b1, b2, op0, op1), [in0, s1, s2], [out])

    def stt(self, out, in0, scalar, in1, op0, op1, e="dve"):
        oa, a0, sa, a1 = out.ap, in0.ap, _ap(scalar), in1.ap
        self.emit(e, lambda v: v.scalar_tensor_tensor(oa, a0, sa, a1, op0, op1), [in0, scalar, in1], [out])

    def copy(self, e, out, in_):
        oa, ia = out.ap, in_.ap
        if e == "act":
            self.emit(e, lambda a: a.copy(oa, ia), [in_], [out])
        else:
            self.emit(e, lambda v: v.tensor_copy(oa, ia), [in_], [out])

    def memset(self, e, out, val):
        oa = out.ap
        self.emit(e, lambda v: v.memset(oa, val), [], [out])

    def finish_prog(self):
        return self.finish(self.final_tiles)

    def finish(self, final_tiles):
        self.wait_all_writes("sp", final_tiles)
        nc = self.nc
        q = self.q

        def run(eng, lst):
            for it in lst:
                if it[0] == "w":
                    eng.wait_ge(it[1], it[2])
                else:
                    ins = it[1](eng)
                    if it[2] is not None:
                        ins.then_inc(it[2], it[3])

        with nc.Block() as block:
            @block.tensor
            def _(t):
                run(t, q["pe"])

            @block.scalar
            def _(a):
                run(a, q["act"])

            @block.vector
            def _(v):
                run(v, q["dve"])

            @block.gpsimd
            def _(g):
                run(g, q["pool"])

            @block.sync
            def _(s):
                run(s, q["sp"])
        self.es.close()
        return nc


NV = 292
V_ADAB, V_N1, V_N2, V_SCW, V_SCB, V_DTB, V_ALOG, V_SD, V_SNG, V_PSC, V_FCW, V_FCB = \
    0, 48, 56, 64, 96, 104, 105, 106, 110, 114, 116, 248
K_FG, K_C, K_IFR, K_SGN, K_INVW, K_INVC, K_ID, K_ONES, K_NEG, K_AM = 0, 8, 16, 17, 18, 20, 52, 180, 308, 692
NCONST = 948
WIN_COLS = IN_W + 512


def _chunks(v, n):
    return np.ascontiguousarray(v.reshape(n, 128).T)


def pack_vecs(inp, l):
    v = np.zeros((128, NV), np.float32)
    v[:, V_ADAB:V_ADAB + 48] = _chunks(inp["ada_b"][l], 48)
    v[:, V_N1:V_N1 + 8] = _chunks(inp["norm1_g"][l], 8)
    v[:, V_N2:V_N2 + 8] = _chunks(inp["norm2_g"][l], 8)
    cw = inp["ssd_conv_w"][l]
    for k in range(8):
        for tap in range(4):
            v[:, V_SCW + k * 4 + tap] = cw[tap, k * 128:(k + 1) * 128]
    v[:, V_SCB:V_SCB + 8] = _chunks(inp["ssd_conv_b"][l], 8)
    v[0:8, V_DTB] = inp["ssd_dt_bias"][l]
    v[0:8, V_ALOG] = inp["ssd_a_log"][l]
    v[:, V_SD:V_SD + 4] = _chunks(np.repeat(inp["ssd_d"][l], 64), 4)
    v[:, V_SNG:V_SNG + 4] = _chunks(inp["ssd_norm_g"][l], 4)
    v[:, V_PSC:V_PSC + 2] = _chunks(inp["pool_scale"][l], 2)
    fw = inp["ffn_conv_w"][l]
    for k in range(44):
        for tap in range(3):
            v[:, V_FCW + k * 3 + tap] = fw[tap, k * 128:(k + 1) * 128]
    v[:, V_FCB:V_FCB + 44] = _chunks(inp["ffn_conv_b"][l], 44)
    return v


def pack_consts(inp, b):
    c = np.zeros((128, NCONST), np.float32)
    c[:, K_FG:K_FG + 8] = _chunks(inp["final_g"], 8)
    c[:, K_C:K_C + 8] = _chunks(inp["c"][b], 8)
    p = np.arange(128)
    e = p % 64
    inv_freq = (np.float32(500000.0) ** (-np.arange(0, 16, 2, dtype=np.float32) / np.float32(16))).astype(np.float32)
    ifr = np.zeros(128, np.float32)
    ifr[e < 16] = inv_freq[e[e < 16] % 8]
    c[:, K_IFR] = ifr
    sgn = np.zeros(128, np.float32)
    sgn[e < 8] = -1.0
    sgn[(e >= 8) & (e < 16)] = 1.0
    c[:, K_SGN] = sgn
    for ch in range(2):
        g = 2 * ch + p // 64
        w = 2.0 ** (g + 1)
        c[:, K_INVW + ch] = 1.0 / w
        for t in range(16):
            c[:, K_INVC + ch * 16 + t] = 1.0 / np.minimum(t + 1, w)
    c[:, K_ID:K_ID + 128] = np.eye(128, dtype=np.float32)
    c[:, K_ONES:K_ONES + 128] = 1.0
    l = np.arange(256)[None, :]
    s = p[:, None]
    neg = np.where(l >= s, 0.0, -30000.0)
    c[:, K_NEG:K_NEG + 256] = neg
    c[:, K_NEG + 256:K_NEG + 384] = neg[:, 0:128]
    qq = np.arange(256)[None, :]
    rel = qq - s
    c[:, K_AM:K_AM + 256] = ((rel >= 0) & (rel <= 128)).astype(np.float32)
    return c


def pack_w_in(w):
    perm = np.arange(256)
    e = perm % 64
    pp = perm.copy()
    pp[e < 8] = perm[e < 8] + 8
    pp[(e >= 8) & (e < 16)] = perm[(e >= 8) & (e < 16)] - 8
    q = w[:, C_Q:C_Q + 256][:, pp]
    k = w[:, C_K:C_K + 256][:, pp]
    return np.ascontiguousarray(np.concatenate([w, q, k], axis=1))


def pack_pool(pw):
    o = np.zeros((128, 2, 128), np.float32)
    for g in range(4):
        ch, h = g // 2, g % 2
        o[h * 64:(h + 1) * 64, ch, h * 64:(h + 1) * 64] = pw[g]
    return o


def _barrier(P):
    for e in ENGS:
        for f in ENGS:
            if f == e or P.cnt[f] == 0:
                continue
            s = P.sem[f]
            if P.waited[e].get(s.num, 0) < P.cnt[f]:
                P.waited[e][s.num] = P.cnt[f]
                P.q[e].append(("w", s, P.cnt[f]))
        for qn in ("sp", "pool"):
            n = P.dcnt[qn]
            for i in range(P.NS):
                c = (n - i + P.NS - 1) // P.NS if n > i else 0
                if c > 0:
                    s = P.dsem[qn][i]
                    if P.waited[e].get(s.num, 0) < 16 * c:
                        P.waited[e][s.num] = 16 * c
                        P.q[e].append(("w", s, 16 * c))


class Arena:
    def __init__(self, P, cols):
        self.P = P
        self.t = P.es.enter_context(P.nc.sbuf_tensor("arena", [128, cols], F32))
        self.cols = cols
        self.off = 0

    def reset(self):
        self.off = 0

    def alloc(self, shape, dt=F32):
        n = 1
        for d in shape[1:]:
            n *= d
        ncols = n if dt in (F32, I32) else (n + 1) // 2
        assert self.off + ncols <= self.cols, ("arena overflow", self.off, ncols, self.cols)
        ap = self.t[:, self.off:self.off + ncols]
        self.off += ncols
        if dt != F32:
            ap = ap.bitcast(dt)
        if len(shape) == 3:
            ap = ap.rearrange("p (k n) -> p k n", k=shape[1])
        elif len(shape) == 4:
            ap = ap.rearrange("p (a b n) -> p a b n", a=shape[1], b=shape[2])
        return Tile(ap)


def rmsnorm_rstd(P, ps_tile, xsq_views, ones_f32, rstd_out, nfeat):
    n = len(xsq_views)
    for i, v in enumerate(xsq_views):
        P.mm(ps_tile, ones_f32, v, start=(i == 0), stop=(i == n - 1), sig=(i == n - 1))
    P.act(rstd_out, ps_tile, AF.Ln, bias=P.eps_ap, scale=1.0 / nfeat)
    P.act(rstd_out, rstd_out, AF.Exp, scale=-0.5)


def build(nlayers=DEPTH, dbg=None, stop_after=None):
    dbg = dbg or set()
    P = Prog()
    P.dbg = set(dbg)
    nc = P.nc
    EI = "ExternalInput"
    xT_in = P.dram("xT", [D, S], F32, EI)
    pos_in = P.dram("pos", [1, S], I32, EI)
    consts_in = P.dram("consts", [128, NCONST], F32, EI)
    vecs_in = P.dram("vecs", [nlayers, 128, NV], F32, EI)
    ada_w = P.dram("ada_w", [nlayers, D, 6 * D], F32, EI)
    w_in = P.dram("w_in", [nlayers, D, WIN_COLS], F32, EI)
    pool_bd = P.dram("pool_bd", [nlayers, 128, 2, 128], F32, EI)
    w_out = P.dram("w_out", [nlayers, D, D], F32, EI)
    ffn_up = P.dram("ffn_up", [nlayers, D, 2 * FFN], F32, EI)
    ffn_down = P.dram("ffn_down", [nlayers, FFN, D], F32, EI)
    outT = P.dram("outT", [D, S], F32, "ExternalOutput")
    dbg_out = {}

    x1_d = P.dram("x1_d", [D, S], F32)
    x2_d = P.dram("x2_d", [D, S], F32)
    zs_d = P.dram("zs_d", [512, S], F32)
    xbc_d = P.dram("xbc_d", [1024, S], F32)
    dt_d = P.dram("dt_d", [8, S], F32)
    acum_d = P.dram("acum_d", [8, S], F32)
    qk_d = P.dram("qk_d", [4, 128, S], BF16)
    v_d = P.dram("v_d", [S, 256], BF16)
    mix_d = P.dram("mix_d", [D, S], BF16)
    act_d = P.dram("act_d", [FFN, S], BF16)

    consts = P.sb("consts", [128, NCONST], F32)
    vecs = P.sb("vecs", [128, DEPTH, NV], F32)
    modv = P.sb("modv", [128, DEPTH, 48], F32)
    dervec = P.sb("dervec", [128, DEPTH, 32], F32)
    epsv = P.sb("epsv", [128, 1], F32)
    ident_bf = P.sb("ident_bf", [128, 128], BF16)
    ones_bf = P.sb("ones_bf", [128, 128], BF16)
    amask_bf = P.sb("amask_bf", [128, 512], BF16)
    ropeC = P.sb("ropeC", [128, S], F32)
    ropeS = P.sb("ropeS", [128, S], F32)
    slab = [P.sb("slab%d" % i, [128, 8, 128], BF16) for i in range(4)]
    pbd_bf = P.sb("pbd_bf", [128, 2, 128], BF16)
    PS = [P.ps("ps%d" % i, [128, 512], F32) for i in range(8)]
    AR = Arena(P, 38400)
    P.eps_ap = epsv[:, 0:1]

    ident = consts[:, K_ID:K_ID + 128]
    ones32 = consts[:, K_ONES:K_ONES + 128]

    P.dma(consts[:, :], consts_in[:, :])
    P.dma(vecs[:, 0:nlayers, :], vecs_in[:, :, :].map(lambda a: a.rearrange("l p n -> p l n")))
    P.memset("dve", epsv[:, :], EPS)
    P.copy("dve", ident_bf[:, :], ident)
    P.copy("dve", ones_bf[:, :], ones32)
    P.copy("dve", amask_bf[:, 0:256], consts[:, K_AM:K_AM + 256])
    P.copy("dve", amask_bf[:, 256:512], consts[:, K_AM:K_AM + 256])

    AR.reset()
    cact = AR.alloc([128, 8])
    P.act(cact[:, :], consts[:, K_C:K_C + 8], AF.Silu)
    adab = [AR.alloc([128, 8, 512]) for _ in range(2)]
    psm = PS[0]
    for l in range(nlayers):
        for pc in range(12):
            buf = adab[pc % 2]
            P.dma(buf[:, :, :], ada_w[l, :, pc * 512:(pc + 1) * 512].map(
                lambda a: a.rearrange("(k p) n -> p k n", p=128)))
            for j in range(4):
                col = pc * 4 + j
                for k in range(8):
                    P.mm(psm[:, col:col + 1], buf[:, k, j * 128:(j + 1) * 128], cact[:, k:k + 1],
                         start=(k == 0), stop=(k == 7), sig=(k == 7))
        P.tt("dve", modv[:, l, :], psm[:, 0:48], vecs[:, l, V_ADAB:V_ADAB + 48], ALU.add)
        P.ts("dve", dervec[:, l, 0:8], modv[:, l, 8:16], 1.0, ALU.add)
        P.tt("dve", dervec[:, l, 0:8], dervec[:, l, 0:8], vecs[:, l, V_N1:V_N1 + 8], ALU.mult)
        P.ts("dve", dervec[:, l, 8:16], modv[:, l, 32:40], 1.0, ALU.add)
        P.tt("dve", dervec[:, l, 8:16], dervec[:, l, 8:16], vecs[:, l, V_N2:V_N2 + 8], ALU.mult)
        P.act(dervec[:, l, 16:17], vecs[:, l, V_ALOG:V_ALOG + 1], AF.Exp)
        P.ts("dve", dervec[:, l, 16:17], dervec[:, l, 16:17], -1.0, ALU.mult)

    TWO_PI = 2.0 * np.pi
    posi = AR.alloc([128, S], I32)
    ang = AR.alloc([128, S])
    tmp = AR.alloc([128, S])
    kf = AR.alloc([128, S])
    P.dma(posi[:, :], pos_in[:, :].map(lambda a: a.broadcast_to([128, S])))
    P.copy("dve", ang[:, :], posi[:, :])
    P.ts("dve", ang[:, :], ang[:, :], consts[:, K_IFR:K_IFR + 1], ALU.mult)
    ki = Tile(posi.t.bitcast(I32)) if False else posi

    def wrap(r):
        P.ts("dve", tmp[:, :], r[:, :], float(np.pi), ALU.is_gt, TWO_PI, ALU.mult)
        P.tt("dve", r[:, :], r[:, :], tmp[:, :], ALU.subtract)
        P.ts("dve", tmp[:, :], r[:, :], float(-np.pi), ALU.is_lt, TWO_PI, ALU.mult)
        P.tt("dve", r[:, :], r[:, :], tmp[:, :], ALU.add)
        P.ts("dve", r[:, :], r[:, :], 3.14159, ALU.min, -3.14159, ALU.max)

    P.ts("dve", tmp[:, :], ang[:, :], float(1.0 / TWO_PI), ALU.mult)
    P.copy("dve", ki[:, :], tmp[:, :])
    P.copy("dve", kf[:, :], ki[:, :])
    P.stt(ang[:, :], kf[:, :], -TWO_PI, ang[:, :], ALU.mult, ALU.add)
    wrap(ang)
    P.act(ropeS[:, :], ang[:, :], AF.Sin, scale=consts[:, K_SGN:K_SGN + 1])
    P.ts("dve", ang[:, :], ang[:, :], float(np.pi / 2), ALU.add)
    wrap(ang)
    P.act(ropeC[:, :], ang[:, :], AF.Sin)
    _barrier(P)

    C = dict(locals())
    xin = xT_in
    for l in range(nlayers):
        xin = layer(P, l, C, xin, last=(l == nlayers - 1), stop_after=stop_after)
    P.final_tiles = [outT] + [C[n] for n in dbg]
    return P


HALO = 16


def phase_norm(P, C, l, xsrc, Aview, Bview, h_full):
    AR = C["AR"]
    PS = C["PS"]
    ones32 = C["ones32"]
    xt = [AR.alloc([128, 8, TT]) for _ in range(2)]
    sq = [AR.alloc([128, 8, TT]) for _ in range(2)]
    rs = [AR.alloc([128, TT]) for _ in range(2)]
    for t in range(NT):
        b = t % 2
        P.dma(xt[b][:, :, :], xsrc[:, t * TT:(t + 1) * TT].map(lambda a: a.rearrange("(k p) n -> p k n", p=128)))
        P.act(sq[b][:, :, :], xt[b][:, :, :], AF.Square)
        rmsnorm_rstd(P, PS[7][:, :], [sq[b][:, k, :] for k in range(8)], ones32, rs[b][:, :], D)
        P.tt("dve", xt[b][:, :, :], xt[b][:, :, :],
             rs[b][:, :].map(lambda a: a.unsqueeze(1).to_broadcast([128, 8, TT])), ALU.mult)
        for k in range(8):
            P.act(h_full[:, k, t * TT:(t + 1) * TT], xt[b][:, k, :], AF.Identity,
                  bias=Bview[k], scale=Aview[k])


def layer(P, l, C, xin, last, stop_after=None):
    AR = C["AR"]
    PS = C["PS"]
    consts, vecs, modv, dervec = C["consts"], C["vecs"], C["modv"], C["dervec"]
    slab = C["slab"]
    w_in = C["w_in"]
    AR.reset()
    h_full = AR.alloc([128, 8, S], BF16)
    mark = AR.off
    phase_norm(P, C, l, xin, [dervec[:, l, k:k + 1] for k in range(8)],
               [modv[:, l, k:k + 1] for k in range(8)], h_full)
    _barrier(P)
    AR.off = mark
    if stop_after == "A0":
        hd = C["mix_d"]
        for k in range(8):
            P.dma(hd[k * 128:(k + 1) * 128, :], h_full[:, k, :])
        return xin

    PB = [AR.alloc([128, HALO + S]) for _ in range(2)]
    OB = [AR.alloc([128, HALO + S]) for _ in range(2)]
    stg = [AR.alloc([128, S], BF16) for _ in range(2)]
    wv = AR.alloc([128, 8, 256], BF16)
    for b_ in PB + OB:
        P.memset("pool", b_[:, 0:HALO], 0.0)
    st = {"slab": 0, "ps": 0, "pb": 0, "ob": 0, "stg": 0}

    def nxt(key, n):
        v = st[key]
        st[key] = (v + 1) % n
        return v

    def load_slab(c0, n):
        sl = slab[nxt("slab", 4)]
        P.dma(sl[:, :, 0:n], w_in[l, :, c0:c0 + n].map(lambda a: a.rearrange("(k p) n -> p k n", p=128)), e="pool")
        return sl

    def project(sl, n, evac):
        for t in range(NT):
            ps = PS[nxt("ps", 6)]
            for k in range(8):
                P.mm(ps[0:n, :], sl[:, k, 0:n], h_full[:, k, t * TT:(t + 1) * TT],
                     start=(k == 0), stop=(k == 7), sig=(k == 7))
            evac(t, ps)

    def tsl(t):
        return slice(HALO + t * TT, HALO + (t + 1) * TT)

    sl = load_slab(C_DT, 8)
    pb = PB[nxt("pb", 2)]
    project(sl, 8, lambda t, ps: P.copy("act", pb[0:8, tsl(t)], ps[0:8, :]))
    ob = OB[nxt("ob", 2)]
    ob2 = OB[nxt("ob", 2)]
    xv = pb[0:8, HALO:HALO + S]
    t1 = ob[0:8, HALO:HALO + S]
    t2 = ob2[0:8, HALO:HALO + S]
    dtb = vecs[0:8, l, V_DTB:V_DTB + 1]
    P.ts("dve", xv, xv, dtb, ALU.add)
    P.ts("dve", t1, xv, -1.0, ALU.mult)
    P.tt("dve", t1, t1, xv, ALU.max)
    P.act(t1, t1, AF.Exp, scale=-1.0)
    P.act(t1, t1, AF.Ln, bias=1.0)
    P.ts("dve", xv, xv, 0.0, ALU.max)
    P.tt("dve", xv, xv, t1, ALU.add)
    P.dma(C["dt_d"][:, :], xv)
    P.ts("dve", t1, xv, dervec[0:8, l, 16:17], ALU.mult)
    P.memset("pool", t2, 1.0)
    P.memset("pool", ob2[0:8, HALO:HALO + S:256], 0.0)
    pb2 = PB[nxt("pb", 2)]
    ac = pb2[0:8, HALO:HALO + S]
    t1a, t2a, aca = t1.ap, t2.ap, ac.ap
    P.emit("dve", lambda v: v.tensor_tensor_scan(aca, t2a, t1a, 0.0, ALU.mult, ALU.add), [t1, t2], [ac])
    P.dma(C["acum_d"][:, :], ac)

    for c in range(4):
        sl = load_slab(C_Z + c * 128, 128)
        ob = OB[nxt("ob", 2)]
        project(sl, 128, lambda t, ps, ob=ob: P.act(ob[:, tsl(t)], ps[:, :], AF.Silu))
        P.dma(C["zs_d"][c * 128:(c + 1) * 128, :], ob[:, HALO:HALO + S])

    for c in range(8):
        sl = load_slab(C_XBC + c * 128, 128)
        pb = PB[nxt("pb", 2)]
        project(sl, 128, lambda t, ps, pb=pb: P.copy("act", pb[:, tsl(t)], ps[:, :]))
        ob = OB[nxt("ob", 2)]
        o = ob[:, HALO:HALO + S]
        w = lambda tap: vecs[:, l, V_SCW + c * 4 + tap:V_SCW + c * 4 + tap + 1]
        P.act(o, pb[:, HALO:HALO + S], AF.Identity, bias=vecs[:, l, V_SCB + c:V_SCB + c + 1], scale=w(3))
        for tap in (2, 1, 0):
            sh = 3 - tap
            P.stt(o, pb[:, HALO - sh:HALO - sh + S], w(tap), o, ALU.mult, ALU.add)
        P.act(o, o, AF.Silu)
        P.dma(C["xbc_d"][c * 128:(c + 1) * 128, :], o)

    for c in range(2):
        sl = load_slab(C_U + c * 128, 128)
        pb = PB[nxt("pb", 2)]
        project(sl, 128, lambda t, ps, pb=pb: P.copy("act", pb[:, tsl(t)], ps[:, :]))
        X1 = OB[0]
        X2 = OB[1]
        X3 = PB[nxt("pb", 2)]

        def shadd(e, dst, src, sh, lo=0, hi=128):
            P.tt(e, dst[lo:hi, HALO:HALO + S], src[lo:hi, HALO:HALO + S], src[lo:hi, HALO - sh:HALO - sh + S], ALU.add)

        shadd("dve", X1, pb, 1)
        if c == 0:
            shadd("dve", X2, X1, 2, 64, 128)
            P.copy("pool", X2[0:64, HALO:HALO + S], X1[0:64, HALO:HALO + S])
            W = X2
        else:
            shadd("dve", X2, X1, 2)
            shadd("dve", X3, X2, 4)
            shadd("dve", X1, X3, 8, 64, 128)
            P.copy("pool", X1[0:64, HALO:HALO + S], X3[0:64, HALO:HALO + S])
            W = X1
        P.tt("dve", W[:, HALO:HALO + 16], W[:, HALO:HALO + 16], consts[:, K_INVC + c * 16:K_INVC + (c + 1) * 16], ALU.mult)
        P.ts("dve", W[:, HALO + 16:HALO + S], W[:, HALO + 16:HALO + S], consts[:, K_INVW + c:K_INVW + c + 1], ALU.mult)
        sg = stg[nxt("stg", 2)]
        P.tt("dve", sg[:, :], W[:, HALO:HALO + S], pb[:, HALO:HALO + S], ALU.subtract)
        if c == 0:
            pbt = AR.alloc([128, 2, 128])
            P.dma(pbt[:, :, :], C["pool_bd"][l, :, :, :])
            P.copy("dve", C["pbd_bf"][:, :, :], pbt[:, :, :])
        so = stg[nxt("stg", 2)]
        for t in range(NT):
            ps = PS[nxt("ps", 6)]
            P.mm(ps[:, :], C["pbd_bf"][:, c, :], sg[:, t * TT:(t + 1) * TT])
            P.act(so[:, t * TT:(t + 1) * TT], ps[:, :], AF.Identity, scale=vecs[:, l, V_PSC + c:V_PSC + c + 1])
        P.dma(C["mix_d"][512 + c * 128:512 + (c + 1) * 128, :], so[:, :])

    for qi, (c0, cp0) in enumerate([(C_Q, IN_W), (C_Q + 128, IN_W + 128), (C_K, IN_W + 256), (C_K + 128, IN_W + 384)]):
        sl = load_slab(c0, 128)
        pa = PB[nxt("pb", 2)]
        project(sl, 128, lambda t, ps, pa=pa: P.copy("act", pa[:, tsl(t)], ps[:, :]))
        sl = load_slab(cp0, 128)
        pp = PB[nxt("pb", 2)]
        project(sl, 128, lambda t, ps, pp=pp: P.copy("act", pp[:, tsl(t)], ps[:, :]))
        ob = OB[nxt("ob", 2)]
        o = ob[:, HALO:HALO + S]
        P.tt("dve", o, pa[:, HALO:HALO + S], C["ropeC"][:, :], ALU.mult)
        P.tt("pool", pp[:, HALO:HALO + S], pp[:, HALO:HALO + S], C["ropeS"][:, :], ALU.mult)
        sg = stg[nxt("stg", 2)]
        P.tt("dve", sg[:, :], o, pp[:, HALO:HALO + S], ALU.add)
        P.dma(C["qk_d"][qi, :, :], sg[:, :])

    P.dma(wv[:, :, :], w_in[l, :, C_V:C_V + 256].map(lambda a: a.rearrange("(k p) n -> p k n", p=128)), e="pool")
    for tb2 in range(16):
        ps = PS[nxt("ps", 6)]
        for h in range(2):
            tb = tb2 * 2 + h
            for k in range(8):
                P.mm(ps[:, h * 256:(h + 1) * 256], h_full[:, k, tb * 128:(tb + 1) * 128], wv[:, k, :],
                     start=(k == 0), stop=(k == 7), sig=(k == 7))
        sg = stg[nxt("stg", 2)]
        P.copy("act", sg[:, 0:512], ps[:, :])
        P.dma(C["v_d"][tb2 * 256:(tb2 + 1) * 256, :].map(lambda a: a.rearrange("(b p) c -> p b c", p=128)),
              sg[:, 0:512].map(lambda a: a.rearrange("p (b c) -> p b c", b=2)))
    _barrier(P)
    if stop_after == "A1":
        return xin
    phase_ssd(P, C, l)
    if stop_after == "A2":
        return xin
    phase_attn(P, C, l)
    if stop_after == "B":
        return xin
    return phase_ffn(P, C, l, xin, last)


def make_in_maps(inp, cores, nl=DEPTH):
    shared = {
        "vecs": np.stack([pack_vecs(inp, l) for l in range(DEPTH)]),
        "ada_w": np.ascontiguousarray(inp["ada_w"], dtype=np.float32),
        "w_in": np.stack([pack_w_in(np.asarray(inp["w_in"][l], np.float32)) for l in range(DEPTH)]),
        "pool_bd": np.stack([pack_pool(np.asarray(inp["pool_w"][l], np.float32)) for l in range(DEPTH)]),
        "w_out": np.ascontiguousarray(inp["w_out"], dtype=np.float32),
        "ffn_up": np.ascontiguousarray(inp["ffn_up"], dtype=np.float32),
        "ffn_down": np.ascontiguousarray(inp["ffn_down"], dtype=np.float32),
    }
    shared = {k: np.ascontiguousarray(v[0:nl]) for k, v in shared.items()}
    maps = []
    for b in cores:
        m = dict(shared)
        m["xT"] = np.ascontiguousarray(np.asarray(inp["x"][b], np.float32).T)
        m["pos"] = np.ascontiguousarray(np.asarray(inp["positions"][b], np.int32).reshape(1, S))
        m["consts"] = pack_consts(inp, b)
        maps.append(m)
    return maps


_CACHE = {}


def kernel(**inputs):
    inp = {k: np.asarray(v) for k, v in inputs.items()}
    if "prog" not in _CACHE:
        _CACHE["prog"] = build().finish_prog()
    nc = _CACHE["prog"]
    maps = make_in_maps(inp, list(range(8)))
    res = run_bass_kernel_spmd(nc, maps, core_ids=list(range(8)))
    out = np.stack([np.ascontiguousarray(res.results[b]["outT"].T) for b in range(8)])
    return out.astype(np.float32)


def phase_ssd(P, C, l):
    AR, PS = C["AR"], C["PS"]
    consts, vecs, dervec = C["consts"], C["vecs"], C["dervec"]
    ident, ones32 = C["ident"], C["ones32"]
    AR.reset()
    xb = [AR.alloc([128, 8, 256]) for _ in range(2)]
    zb = [AR.alloc([128, 4, 256]) for _ in range(2)]
    acb = [AR.alloc([128, 8, 256]) for _ in range(2)]
    dA = [AR.alloc([128, 2, 256]) for _ in range(2)]
    ebc = AR.alloc([128, 8, 256])
    tk = AR.alloc([128, 2, 2, 8])
    dec = AR.alloc([128, 2, 8])
    dd = AR.alloc([128, 2, 8])
    bcT = AR.alloc([128, 4, 256], BF16)
    cms = AR.alloc([128, 8, 256], BF16)
    xdt = AR.alloc([128, 2, 512], BF16)
    xdec = AR.alloc([128, 2, 512], BF16)
    bmtok = AR.alloc([128, 2, 256], BF16)
    seg = [AR.alloc([128, 384]) for _ in range(2)]
    LT = [AR.alloc([128, 384]) for _ in range(2)]
    MT = [AR.alloc([128, 384], BF16) for _ in range(3)]
    hin = AR.alloc([128, 512])
    hin_bf = [AR.alloc([128, 512], BF16) for _ in range(2)]
    ybuf = AR.alloc([128, 4, 256])
    ysq = AR.alloc([128, 4, 256])
    rs = AR.alloc([128, 2, 256])
    obf = [AR.alloc([128, 4, 256], BF16) for _ in range(2)]
    negm = consts[:, K_NEG:K_NEG + 384]
    P.memset("pool", hin[:, :], 0.0)
    P.memset("pool", hin_bf[0][:, :], 0.0)
    XS0, XS1, BMT, CB0, CB1, Y01, Y23, ST = PS
    for c in range(16):
        b = c % 2
        tok = slice(c * 256, (c + 1) * 256)
        P.dma(xb[b][:, :, :], C["xbc_d"][:, tok].map(lambda a: a.rearrange("(k p) n -> p k n", p=128)))
        P.dma(zb[b][:, :, :], C["zs_d"][:, tok].map(lambda a: a.rearrange("(k p) n -> p k n", p=128)))
        P.dma(acb[b][:, :, :], C["acum_d"][:, tok].map(lambda a: a.unsqueeze(0).broadcast_to([128, 8, 256])))
        P.dma(dA[b][0:8, 0, :], C["dt_d"][:, tok])
        P.dma(dA[b][0:8, 1, :], C["acum_d"][:, tok])
        x_, a_ = xb[b], acb[b]
        for sb in range(2):
            for wh in range(2):
                o = (sb * 2 + wh) * 8
                P.mm(CB1[:, o:o + 8], dA[b][0:8, wh, sb * 128:(sb + 1) * 128], ident.map(lambda a: a[0:8, 0:8]))
        P.copy("dve", tk[:, :, :, :], CB1[:, 0:32].map(lambda a: a.rearrange("p (s w h) -> p s w h", s=2, w=2)))
        P.tt("dve", dec[:, :, :], a_[:, :, 255].map(lambda a: a.unsqueeze(1).to_broadcast([128, 2, 8])),
             tk[:, :, 1, :], ALU.subtract)
        P.act(dec[:, :, :], dec[:, :, :], AF.Exp)
        P.tt("dve", dd[:, :, :], dec[:, :, :], tk[:, :, 0, :], ALU.mult)
        P.act(ebc[:, :, :], a_[:, :, :], AF.Exp)
        P.copy("pool", bcT[:, :, :], x_[:, 4:8, :])
        for g in range(2):
            P.tt("pool", cms[:, 4 * g:4 * g + 4, :],
                 x_[:, 6 + g, :].map(lambda a: a.unsqueeze(1).to_broadcast([128, 4, 256])),
                 ebc[:, 4 * g:4 * g + 4, :], ALU.mult)
        for sb, XS in enumerate((XS0, XS1)):
            for ch in range(4):
                P.transpose(XS[:, ch * 128:(ch + 1) * 128], x_[:, ch, sb * 128:(sb + 1) * 128], ident, sig=(ch == 3))
            xs3 = XS[:, :].map(lambda a: a.rearrange("p (h e) -> p h e", h=8))
            P.tt("dve", xdt[:, sb, :].map(lambda a: a.rearrange("p (h e) -> p h e", h=8)), xs3,
                 tk[:, sb, 0, :].map(lambda a: a.unsqueeze(2).to_broadcast([128, 8, 64])), ALU.mult)
            P.tt("dve", xdec[:, sb, :].map(lambda a: a.rearrange("p (h e) -> p h e", h=8)), xs3,
                 dd[:, sb, :].map(lambda a: a.unsqueeze(2).to_broadcast([128, 8, 64])), ALU.mult)
            for g in range(2):
                P.transpose(BMT[:, sb * 256 + g * 128:sb * 256 + (g + 1) * 128], x_[:, 4 + g, sb * 128:(sb + 1) * 128],
                            ident, sig=(g == 1))
        P.copy("act", bmtok[:, :, :], BMT[:, :].map(lambda a: a.rearrange("p (s n) -> p s n", s=2)))
        hb = hin_bf[c % 2]
        for g in range(2):
            CB = CB0 if g == 0 else CB1
            P.mm(CB[:, 0:256], bcT[:, g, 0:128], bcT[:, 2 + g, 0:256])
            P.mm(CB[:, 256:384], bcT[:, g, 128:256], bcT[:, 2 + g, 128:256])
            for j in range(4):
                h = 4 * g + j
                sg_, lt_, mt_ = seg[h % 2], LT[h % 2], MT[h % 3]
                P.stt(sg_[:, 0:256], a_[:, h, 0:256], tk[:, 0, 1, h:h + 1], negm.map(lambda a: a[:, 0:256]),
                      ALU.subtract, ALU.add)
                P.stt(sg_[:, 256:384], a_[:, h, 128:256], tk[:, 1, 1, h:h + 1], negm.map(lambda a: a[:, 256:384]),
                      ALU.subtract, ALU.add)
                P.act(lt_[:, :], sg_[:, :], AF.Exp)
                P.tt("dve", mt_[:, :], CB[:, 0:384], lt_[:, :], ALU.mult)
                Y = Y01 if h < 4 else Y23
                col = ((h // 2) % 2) * 256
                r0 = (h % 2) * 64
                yv = lambda a, b_: Y[r0:r0 + 64, col + a:col + b_]
                P.mm(yv(0, 256), xdt[:, 0, h * 64:(h + 1) * 64], mt_[:, 0:256], start=True, stop=False, sig=False)
                P.mm(yv(128, 256), xdt[:, 1, h * 64:(h + 1) * 64], mt_[:, 256:384], start=False, stop=(c == 0),
                     sig=(c == 0))
                if c > 0:
                    P.mm(yv(0, 256), hb[:, h * 64:(h + 1) * 64], cms[:, h, :], start=False, stop=True)
                for sb in range(2):
                    P.mm(ST[:, h * 64:(h + 1) * 64], bmtok[:, sb, g * 128:(g + 1) * 128],
                         xdec[:, sb, h * 64:(h + 1) * 64], start=(sb == 0), stop=(sb == 1), sig=(sb == 1))
        h3 = lambda t_: t_[:, :].map(lambda a: a.rearrange("p (h e) -> p h e", h=8))
        P.tt("dve", h3(hin), h3(hin), ebc[:, :, 255].map(lambda a: a.unsqueeze(2).to_broadcast([128, 8, 64])), ALU.mult)
        P.tt("dve", hin[:, :], hin[:, :], ST[:, :], ALU.add)
        P.copy("pool", hin_bf[(c + 1) % 2][:, :], hin[:, :])
        for kk in range(4):
            Y = Y01 if kk < 2 else Y23
            P.stt(ybuf[:, kk, :], x_[:, kk, :], vecs[:, l, V_SD + kk:V_SD + kk + 1],
                  Y[:, (kk % 2) * 256:(kk % 2 + 1) * 256], ALU.mult, ALU.add)
        P.tt("dve", ybuf[:, :, :], ybuf[:, :, :], zb[b][:, :, :], ALU.mult)
        P.act(ysq[:, :, :], ybuf[:, :, :], AF.Square)
        for g in range(2):
            rmsnorm_rstd(P, XS0[:, g * 256:(g + 1) * 256], [ysq[:, 2 * g, :], ysq[:, 2 * g + 1, :]], ones32,
                         rs[:, g, :], 256)
        ob = obf[b]
        for kk in range(4):
            P.stt(ob[:, kk, :], ybuf[:, kk, :], vecs[:, l, V_SNG + kk:V_SNG + kk + 1], rs[:, kk // 2, :],
                  ALU.mult, ALU.mult)
        P.dma(C["mix_d"][0:512, tok].map(lambda a: a.rearrange("(k p) n -> p k n", p=128)), ob[:, :, :])
    _barrier(P)


def phase_attn(P, C, l):
    AR, PS = C["AR"], C["PS"]
    AR.reset()
    qk = [AR.alloc([128, S], BF16) for _ in range(4)]
    acc = AR.alloc([128, 2, S])
    NB = 4
    vraw = [AR.alloc([128, 256], BF16) for _ in range(NB)]
    V1 = [AR.alloc([128, 4, 128], BF16) for _ in range(NB)]
    PT = [AR.alloc([128, 2, 256], BF16) for _ in range(3)]
    onesA = AR.alloc([128, 128], BF16)
    onesB = AR.alloc([128, 128], BF16)
    rec = AR.alloc([128, S])
    osb = AR.alloc([128, S], BF16)
    amask = C["amask_bf"][:, :].map(lambda a: a.rearrange("p (h q) -> p h q", h=2))
    for i in range(4):
        P.dma(qk[i][:, :], C["qk_d"][i, :, :])
    for v in V1:
        P.memset("pool", v[:, :, :], 0.0)
    P.memset("pool", onesA[:, :], 0.0)
    P.memset("pool", onesB[:, :], 0.0)
    P.memset("pool", onesA[:, 0:64], 1.0)
    P.memset("pool", onesB[:, 64:128], 1.0)
    it = 0
    qpad = [AR.alloc([128, S], BF16) for _ in range(2)]
    for pair in range(2):
        qT, kT = qk[pair], qk[2 + pair]
        P.memset("pool", acc[:, :, :], 0.0)
        P.memset("pool", qpad[0][64:128, :], 0.0)
        P.memset("pool", qpad[1][0:64, :], 0.0)
        P.copy("pool", qpad[0][0:64, :], qT[0:64, :])
        P.copy("pool", qpad[1][64:128, :], qT[64:128, :])
        for d in (1, 4, 16):
            nb = S // d // 128
            for r in range(d):
                for j in range(nb):
                    nq = 256 if j + 1 < nb else 128
                    k0 = r + d * 128 * j
                    ksl = slice(k0, k0 + d * 127 + 1, d)
                    qsl = slice(k0, k0 + d * (nq - 1) + 1, d)
                    vb = it % NB
                    P.dma(vraw[vb][:, :], C["v_d"][ksl, :])
                    v4 = vraw[vb][:, :].map(lambda a: a.rearrange("p (h e) -> p h e", h=4))
                    P.copy("pool", V1[vb][:, 0:4:2, 0:64], v4.map(lambda a: a[:, 0:4:2, :]))
                    P.copy("pool", V1[vb][:, 1:4:2, 64:128], v4.map(lambda a: a[:, 1:4:2, :]))
                    sp = PS[it % 4]
                    for hh in range(2):
                        P.mm(sp[:, hh * 256:hh * 256 + nq], kT[:, ksl], qpad[hh][:, qsl], sig=(hh == 1))
                    pt = PT[it % 3]
                    sp3 = sp[:, :].map(lambda a: a.rearrange("p (h q) -> p h q", h=2))
                    P.act(pt[:, :, 0:nq], sp3.map(lambda a: a[:, :, 0:nq]), AF.Exp, scale=0.125)
                    P.tt("dve", pt[:, :, 0:nq], pt[:, :, 0:nq], amask.map(lambda a: a[:, :, 0:nq]), ALU.mult)
                    op = PS[4 + it % 4]
                    hA, hB = 2 * pair, 2 * pair + 1
                    P.mm(op[:, 0:nq], V1[vb][:, hA, :], pt[:, 0, 0:nq], start=True, stop=False, sig=False)
                    P.mm(op[:, 0:nq], V1[vb][:, hB, :], pt[:, 1, 0:nq], start=False, stop=True, sig=False)
                    P.mm(op[:, 256:256 + nq], onesA[:, :], pt[:, 0, 0:nq], start=True, stop=False, sig=False)
                    P.mm(op[:, 256:256 + nq], onesB[:, :], pt[:, 1, 0:nq], start=False, stop=True, sig=True)
                    op3 = op[:, :].map(lambda a: a.rearrange("p (h q) -> p h q", h=2))
                    av = acc[:, :, qsl]
                    P.tt("dve", av, av, op3.map(lambda a: a[:, :, 0:nq]), ALU.add)
                    it += 1
        rca, aa = rec[:, :].ap, acc[:, 1, :].ap
        P.emit("dve", lambda v: v.reciprocal(rca, aa), [acc], [rec])
        P.tt("dve", osb[:, :], acc[:, 0, :], rec[:, :], ALU.mult)
        P.dma(C["mix_d"][768 + pair * 128:768 + (pair + 1) * 128, :], osb[:, :])
    _barrier(P)


def phase_ffn(P, C, l, xin, last):
    AR, PS = C["AR"], C["PS"]
    consts, vecs, modv, dervec, slab = C["consts"], C["vecs"], C["modv"], C["dervec"], C["slab"]
    st = {"slab": 0, "ps": 0}

    def nxt(key, n):
        v = st[key]
        st[key] = (v + 1) % n
        return v

    AR.reset()
    mixf = AR.alloc([128, 8, S], BF16)
    xrow = [AR.alloc([128, S]) for _ in range(2)]
    for k in range(8):
        P.dma(mixf[:, k, :], C["mix_d"][k * 128:(k + 1) * 128, :])
    for n in range(8):
        sl = slab[nxt("slab", 4)]
        P.dma(sl[:, :, :], C["w_out"][l, :, n * 128:(n + 1) * 128].map(lambda a: a.rearrange("(k p) n -> p k n", p=128)),
              e="pool")
        xr = xrow[n % 2]
        P.dma(xr[:, :], xin[n * 128:(n + 1) * 128, :])
        for t in range(NT):
            ps = PS[nxt("ps", 7)]
            for k in range(8):
                P.mm(ps[:, :], sl[:, k, :], mixf[:, k, t * TT:(t + 1) * TT], start=(k == 0), stop=(k == 7), sig=(k == 7))
            P.stt(xr[:, t * TT:(t + 1) * TT], ps[:, :], modv[:, l, 16 + n:17 + n], xr[:, t * TT:(t + 1) * TT],
                  ALU.mult, ALU.add)
        P.dma(C["x1_d"][n * 128:(n + 1) * 128, :], xr[:, :])
    _barrier(P)
    AR.reset()
    h2 = AR.alloc([128, 8, S], BF16)
    mark = AR.off
    phase_norm(P, C, l, C["x1_d"], [dervec[:, l, 8 + k:9 + k] for k in range(8)],
               [modv[:, l, 24 + k:25 + k] for k in range(8)], h2)
    _barrier(P)
    AR.off = mark
    XG = AR.alloc([128, HALO + S])
    XU = AR.alloc([128, HALO + S])
    YG = AR.alloc([128, S])
    YU = AR.alloc([128, S])
    actb = [AR.alloc([128, S], BF16) for _ in range(2)]
    P.memset("pool", XG[:, 0:HALO], 0.0)
    P.memset("pool", XU[:, 0:HALO], 0.0)
    for i in range(NFC):
        for (X, Yb, cc, col0, ev) in ((XG, YG, i, i * 128, "act"), (XU, YU, NFC + i, FFN + i * 128, "dve")):
            sl = slab[nxt("slab", 4)]
            P.dma(sl[:, :, :], C["ffn_up"][l, :, col0:col0 + 128].map(lambda a: a.rearrange("(k p) n -> p k n", p=128)),
                  e="pool")
            for t in range(NT):
                ps = PS[nxt("ps", 7)]
                for k in range(8):
                    P.mm(ps[:, :], sl[:, k, :], h2[:, k, t * TT:(t + 1) * TT], start=(k == 0), stop=(k == 7),
                         sig=(k == 7))
                P.copy(ev, X[:, HALO + t * TT:HALO + (t + 1) * TT], ps[:, :])
            w = lambda tap, cc=cc: vecs[:, l, V_FCW + cc * 3 + tap:V_FCW + cc * 3 + tap + 1]
            P.act(Yb[:, :], X[:, HALO:HALO + S], AF.Identity, bias=vecs[:, l, V_FCB + cc:V_FCB + cc + 1], scale=w(2))
            P.stt(Yb[:, :], X[:, HALO - 1:HALO - 1 + S], w(1), Yb[:, :], ALU.mult, ALU.add)
            P.stt(Yb[:, :], X[:, HALO - 2:HALO - 2 + S], w(0), Yb[:, :], ALU.mult, ALU.add)
        P.act(YG[:, :], YG[:, :], AF.Silu)
        ab = actb[i % 2]
        P.tt("pool", ab[:, :], YG[:, :], YU[:, :], ALU.mult)
        P.dma(C["act_d"][i * 128:(i + 1) * 128, :], ab[:, :])
    _barrier(P)
    AR.reset()
    wd = AR.alloc([128, NFC, D], BF16)
    at = [AR.alloc([128, NFC, TT], BF16) for _ in range(2)]
    xt = [AR.alloc([128, 8, TT]) for _ in range(2)]
    sq = AR.alloc([128, 8, TT])
    rsf = AR.alloc([128, TT])
    for kq in range(0, NFC, 2):
        P.dma(wd[:, kq:kq + 2, :], C["ffn_down"][l, kq * 128:(kq + 2) * 128, :].map(
            lambda a: a.rearrange("(k p) n -> p k n", p=128)), e="pool")
    xout = C["x2_d"]
    for t in range(NT):
        b = t % 2
        tsl_ = slice(t * TT, (t + 1) * TT)
        P.dma(at[b][:, 0:11, :], C["act_d"][0:11 * 128, tsl_].map(lambda a: a.rearrange("(k p) n -> p k n", p=128)))
        P.dma(at[b][:, 11:22, :], C["act_d"][11 * 128:22 * 128, tsl_].map(lambda a: a.rearrange("(k p) n -> p k n", p=128)))
        P.dma(xt[b][:, :, :], C["x1_d"][:, tsl_].map(lambda a: a.rearrange("(k p) n -> p k n", p=128)))
        for n in range(8):
            ps = PS[nxt("ps", 7)]
            for k in range(NFC):
                P.mm(ps[:, :], wd[:, k, n * 128:(n + 1) * 128], at[b][:, k, :], start=(k == 0), stop=(k == NFC - 1),
                     sig=(k == NFC - 1))
            P.stt(xt[b][:, n, :], ps[:, :], modv[:, l, 40 + n:41 + n], xt[b][:, n, :], ALU.mult, ALU.add)
        if not last:
            P.dma(xout[:, tsl_].map(lambda a: a.rearrange("(k p) n -> p k n", p=128)), xt[b][:, :, :])
        else:
            P.act(sq[:, :, :], xt[b][:, :, :], AF.Square)
            rmsnorm_rstd(P, PS[7][:, :], [sq[:, k, :] for k in range(8)], C["ones32"], rsf[:, :], D)
            P.tt("dve", xt[b][:, :, :], xt[b][:, :, :],
                 rsf[:, :].map(lambda a: a.unsqueeze(1).to_broadcast([128, 8, TT])), ALU.mult)
            P.tt("dve", xt[b][:, :, :], xt[b][:, :, :],
                 consts[:, K_FG:K_FG + 8].map(lambda a: a.unsqueeze(2).to_broadcast([128, 8, TT])), ALU.mult)
            P.dma(C["outT"][:, tsl_].map(lambda a: a.rearrange("(k p) n -> p k n", p=128)), xt[b][:, :, :])
    _barrier(P)
    return xout
```

```python
import numpy as np
from contextlib import ExitStack
import concourse.bass as bass
import concourse.mybir as mybir
from concourse.bass_utils import run_bass_kernel_spmd

F32 = mybir.dt.float32
BF16 = mybir.dt.bfloat16
I32 = mybir.dt.int32
ALU = mybir.AluOpType
AF = mybir.ActivationFunctionType
AX = mybir.AxisListType

S = 4096
D = 1024
DEPTH = 2
NT = 8
TT = 512
FFN = 2816
NFC = 22
EPS = 1e-6
IN_W = 2568
C_Z, C_XBC, C_DT, C_U, C_Q, C_K, C_V = 0, 512, 1536, 1544, 1800, 2056, 2312

SAME_SYNC = True
ENGS = ("pe", "act", "dve", "pool", "sp")


class View:
    __slots__ = ("tile", "ap")

    def __init__(self, tile, ap):
        self.tile = tile
        self.ap = ap

    def map(self, f):
        return View(self.tile, f(self.ap))


class Tile:
    def __init__(self, t):
        self.t = t
        self.w = {}
        self.r = {}

    def __getitem__(self, k):
        return View(self, self.t[k])


def _tiles(vs):
    out = []
    for v in vs:
        if isinstance(v, View):
            out.append(v.tile)
        elif isinstance(v, Tile):
            out.append(v)
    return out


def _ap(v):
    return v.ap if isinstance(v, View) else v


class Prog:
    NS = 8

    def __init__(self):
        self.nc = bass.Bass("TRN2", target_bir_lowering=False)
        self.es = ExitStack()
        nc = self.nc
        self.q = {e: [] for e in ENGS}
        self.cnt = {e: 0 for e in ENGS}
        self.sem = {e: self.es.enter_context(nc.semaphore("s_" + e)) for e in ENGS}
        self.dsem = {e: [self.es.enter_context(nc.semaphore("d_%s%d" % (e, i))) for i in range(self.NS)]
                     for e in ("sp", "pool")}
        self.dcnt = {"sp": 0, "pool": 0}
        self.waited = {e: {} for e in ENGS}
        self.ninst = 0
        self.dbg = set()

    def sb(self, name, shape, dt):
        return Tile(self.es.enter_context(self.nc.sbuf_tensor("sb_" + name, list(shape), dt)))

    def ps(self, name, shape, dt=F32):
        return Tile(self.es.enter_context(self.nc.psum_tensor(name, list(shape), dt)))

    def dram(self, name, shape, dt, kind="Internal"):
        if name in self.dbg:
            kind = "ExternalOutput"
        return Tile(self.nc.dram_tensor(name, list(shape), dt, kind=kind).ap())

    def _deps(self, e, reads, writes):
        need = {}

        def add(d):
            for s, (sem, v) in d.items():
                if s not in need or need[s][1] < v:
                    need[s] = (sem, v)

        for t in reads:
            add(t.w)
        for t in writes:
            add(t.w)
            add(t.r)
        own = self.sem[e].num
        for s, (sem, v) in need.items():
            if s == own and (e == "pe" or not SAME_SYNC):
                continue
            if self.waited[e].get(s, 0) >= v:
                continue
            self.waited[e][s] = v
            self.q[e].append(("w", sem, v))

    def _mark(self, tok, reads, writes):
        sem, v = tok
        for t in reads:
            if t.r.get(sem.num, (None, 0))[1] < v:
                t.r[sem.num] = tok
        for t in writes:
            t.r = {}
            if t.w.get(sem.num, (None, 0))[1] < v:
                t.w[sem.num] = tok

    def emit(self, e, fn, reads=(), writes=(), sig=True):
        reads = _tiles(reads)
        writes = _tiles(writes)
        self._deps(e, reads, writes)
        sem = self.sem[e]
        if sig:
            self.cnt[e] += 1
            val = self.cnt[e]
        else:
            val = self.cnt[e] + 1
        self.q[e].append(("i", fn, sem if sig else None, 1))
        self._mark((sem, val), reads, writes)
        self.ninst += 1

    def dma(self, out, in_, e="sp", **kw):
        j = self.dcnt[e]
        self.dcnt[e] += 1
        dsem = self.dsem[e][j % self.NS]
        if j >= self.NS:
            v = 16 * (j // self.NS)
            if self.waited[e].get(dsem.num, 0) < v:
                self.waited[e][dsem.num] = v
                self.q[e].append(("w", dsem, v))
        reads = _tiles([in_])
        writes = _tiles([out])
        self._deps(e, reads, writes)
        oa, ia = out.ap, in_.ap
        self.q[e].append(("i", lambda eng: eng.dma_start(out=oa, in_=ia, **kw), dsem, 16))
        self._mark((dsem, 16 * (j // self.NS + 1)), reads, writes)
        self.ninst += 1

    def wait_all_writes(self, e, tiles):
        self._deps(e, _tiles(tiles), [])

    def eng(self, e):
        return e

    def mm(self, out, lhsT, rhs, start=True, stop=True, sig=True):
        oa, la, ra = out.ap, lhsT.ap, rhs.ap
        self.emit("pe", lambda t: t.matmul(oa, la, ra, start=start, stop=stop), [lhsT, rhs], [out], sig)

    def transpose(self, out, in_, ident, sig=True):
        oa, ia, da = out.ap, in_.ap, ident.ap
        self.emit("pe", lambda t: t.transpose(oa, ia, da), [in_, ident], [out], sig)

    def act(self, out, in_, func, bias=0.0, scale=1.0, e="act"):
        oa, ia, ba, sa = out.ap, in_.ap, _ap(bias), _ap(scale)
        self.emit("act", lambda a: a.activation(oa, ia, func, bias=ba, scale=sa), [in_, bias, scale], [out])

    def tt(self, e, out, in0, in1, op):
        oa, a0, a1 = out.ap, in0.ap, in1.ap
        self.emit(e, lambda v: v.tensor_tensor(oa, a0, a1, op), [in0, in1], [out])

    def ts(self, e, out, in0, s1, op0, s2=None, op1=ALU.bypass):
        oa, a0, b1, b2 = out.ap, in0.ap, _ap(s1), _ap(s2)
        self.emit(e, lambda v: v.tensor_scalar(oa, a0, b1, b2, op0, op1), [in0, s1, s2], [out])

    def stt(self, out, in0, scalar, in1, op0, op1, e="dve"):
        oa, a0, sa, a1 = out.ap, in0.ap, _ap(scalar), in1.ap
        self.emit(e, lambda v: v.scalar_tensor_tensor(oa, a0, sa, a1, op0, op1), [in0, scalar, in1], [out])

    def copy(self, e, out, in_):
        oa, ia = out.ap, in_.ap
        if e == "act":
            self.emit(e, lambda a: a.copy(oa, ia), [in_], [out])
        else:
            self.emit(e, lambda v: v.tensor_copy(oa, ia), [in_], [out])

    def memset(self, e, out, val):
        oa = out.ap
        self.emit(e, lambda v: v.memset(oa, val), [], [out])

    def finish_prog(self):
        return self.finish(self.final_tiles)

    def finish(self, final_tiles):
        self.wait_all_writes("sp", final_tiles)
        nc = self.nc
        q = self.q

        def run(eng, lst):
            for it in lst:
                if it[0] == "w":
                    eng.wait_ge(it[1], it[2])
                else:
                    ins = it[1](eng)
                    if it[2] is not None:
                        ins.then_inc(it[2], it[3])

        with nc.Block() as block:
            @block.tensor
            def _(t):
                run(t, q["pe"])

            @block.scalar
            def _(a):
                run(a, q["act"])

            @block.vector
            def _(v):
                run(v, q["dve"])

            @block.gpsimd
            def _(g):
                run(g, q["pool"])

            @block.sync
            def _(s):
                run(s, q["sp"])
        self.es.close()
        return nc


NV = 292
V_ADAB, V_N1, V_N2, V_SCW, V_SCB, V_DTB, V_ALOG, V_SD, V_SNG, V_PSC, V_FCW, V_FCB = \
    0, 48, 56, 64, 96, 104, 105, 106, 110, 114, 116, 248
K_FG, K_C, K_IFR, K_SGN, K_INVW, K_INVC, K_ID, K_ONES, K_NEG, K_AM = 0, 8, 16, 17, 18, 20, 52, 180, 308, 692
NCONST = 948
WIN_COLS = IN_W + 512


def _chunks(v, n):
    return np.ascontiguousarray(v.reshape(n, 128).T)


def pack_vecs(inp, l):
    v = np.zeros((128, NV), np.float32)
    v[:, V_ADAB:V_ADAB + 48] = _chunks(inp["ada_b"][l], 48)
    v[:, V_N1:V_N1 + 8] = _chunks(inp["norm1_g"][l], 8)
    v[:, V_N2:V_N2 + 8] = _chunks(inp["norm2_g"][l], 8)
    cw = inp["ssd_conv_w"][l]
    for k in range(8):
        for tap in range(4):
            v[:, V_SCW + k * 4 + tap] = cw[tap, k * 128:(k + 1) * 128]
    v[:, V_SCB:V_SCB + 8] = _chunks(inp["ssd_conv_b"][l], 8)
    v[0:8, V_DTB] = inp["ssd_dt_bias"][l]
    v[0:8, V_ALOG] = inp["ssd_a_log"][l]
    v[:, V_SD:V_SD + 4] = _chunks(np.repeat(inp["ssd_d"][l], 64), 4)
    v[:, V_SNG:V_SNG + 4] = _chunks(inp["ssd_norm_g"][l], 4)
    v[:, V_PSC:V_PSC + 2] = _chunks(inp["pool_scale"][l], 2)
    fw = inp["ffn_conv_w"][l]
    for k in range(44):
        for tap in range(3):
            v[:, V_FCW + k * 3 + tap] = fw[tap, k * 128:(k + 1) * 128]
    v[:, V_FCB:V_FCB + 44] = _chunks(inp["ffn_conv_b"][l], 44)
    return v


def pack_consts(inp, b):
    c = np.zeros((128, NCONST), np.float32)
    c[:, K_FG:K_FG + 8] = _chunks(inp["final_g"], 8)
    c[:, K_C:K_C + 8] = _chunks(inp["c"][b], 8)
    p = np.arange(128)
    e = p % 64
    inv_freq = (np.float32(500000.0) ** (-np.arange(0, 16, 2, dtype=np.float32) / np.float32(16))).astype(np.float32)
    ifr = np.zeros(128, np.float32)
    ifr[e < 16] = inv_freq[e[e < 16] % 8]
    c[:, K_IFR] = ifr
    sgn = np.zeros(128, np.float32)
    sgn[e < 8] = -1.0
    sgn[(e >= 8) & (e < 16)] = 1.0
    c[:, K_SGN] = sgn
    for ch in range(2):
        g = 2 * ch + p // 64
        w = 2.0 ** (g + 1)
        c[:, K_INVW + ch] = 1.0 / w
        for t in range(16):
            c[:, K_INVC + ch * 16 + t] = 1.0 / np.minimum(t + 1, w)
    c[:, K_ID:K_ID + 128] = np.eye(128, dtype=np.float32)
    c[:, K_ONES:K_ONES + 128] = 1.0
    l = np.arange(256)[None, :]
    s = p[:, None]
    neg = np.where(l >= s, 0.0, -30000.0)
    c[:, K_NEG:K_NEG + 256] = neg
    c[:, K_NEG + 256:K_NEG + 384] = neg[:, 0:128]
    qq = np.arange(256)[None, :]
    rel = qq - s
    c[:, K_AM:K_AM + 256] = ((rel >= 0) & (rel <= 128)).astype(np.float32)
    return c


def pack_w_in(w):
    perm = np.arange(256)
    e = perm % 64
    pp = perm.copy()
    pp[e < 8] = perm[e < 8] + 8
    pp[(e >= 8) & (e < 16)] = perm[(e >= 8) & (e < 16)] - 8
    q = w[:, C_Q:C_Q + 256][:, pp]
    k = w[:, C_K:C_K + 256][:, pp]
    return np.ascontiguousarray(np.concatenate([w, q, k], axis=1))


def pack_pool(pw):
    o = np.zeros((128, 2, 128), np.float32)
    for g in range(4):
        ch, h = g // 2, g % 2
        o[h * 64:(h + 1) * 64, ch, h * 64:(h + 1) * 64] = pw[g]
    return o


def _barrier(P):
    for e in ENGS:
        for f in ENGS:
            if f == e or P.cnt[f] == 0:
                continue
            s = P.sem[f]
            if P.waited[e].get(s.num, 0) < P.cnt[f]:
                P.waited[e][s.num] = P.cnt[f]
                P.q[e].append(("w", s, P.cnt[f]))
        for qn in ("sp", "pool"):
            n = P.dcnt[qn]
            for i in range(P.NS):
                c = (n - i + P.NS - 1) // P.NS if n > i else 0
                if c > 0:
                    s = P.dsem[qn][i]
                    if P.waited[e].get(s.num, 0) < 16 * c:
                        P.waited[e][s.num] = 16 * c
                        P.q[e].append(("w", s, 16 * c))


class Arena:
    def __init__(self, P, cols):
        self.P = P
        self.t = P.es.enter_context(P.nc.sbuf_tensor("arena", [128, cols], F32))
        self.cols = cols
        self.off = 0

    def reset(self):
        self.off = 0

    def alloc(self, shape, dt=F32):
        n = 1
        for d in shape[1:]:
            n *= d
        ncols = n if dt in (F32, I32) else (n + 1) // 2
        assert self.off + ncols <= self.cols, ("arena overflow", self.off, ncols, self.cols)
        ap = self.t[:, self.off:self.off + ncols]
        self.off += ncols
        if dt != F32:
            ap = ap.bitcast(dt)
        if len(shape) == 3:
            ap = ap.rearrange("p (k n) -> p k n", k=shape[1])
        elif len(shape) == 4:
            ap = ap.rearrange("p (a b n) -> p a b n", a=shape[1], b=shape[2])
        return Tile(ap)


def rmsnorm_rstd(P, ps_tile, xsq_views, ones_f32, rstd_out, nfeat):
    n = len(xsq_views)
    for i, v in enumerate(xsq_views):
        P.mm(ps_tile, ones_f32, v, start=(i == 0), stop=(i == n - 1), sig=(i == n - 1))
    P.act(rstd_out, ps_tile, AF.Ln, bias=P.eps_ap, scale=1.0 / nfeat)
    P.act(rstd_out, rstd_out, AF.Exp, scale=-0.5)


def build(nlayers=DEPTH, dbg=None, stop_after=None):
    dbg = dbg or set()
    P = Prog()
    P.dbg = set(dbg)
    nc = P.nc
    EI = "ExternalInput"
    xT_in = P.dram("xT", [D, S], F32, EI)
    pos_in = P.dram("pos", [1, S], I32, EI)
    consts_in = P.dram("consts", [128, NCONST], F32, EI)
    vecs_in = P.dram("vecs", [nlayers, 128, NV], F32, EI)
    ada_w = P.dram("ada_w", [nlayers, D, 6 * D], F32, EI)
    w_in = P.dram("w_in", [nlayers, D, WIN_COLS], F32, EI)
    pool_bd = P.dram("pool_bd", [nlayers, 128, 2, 128], F32, EI)
    w_out = P.dram("w_out", [nlayers, D, D], F32, EI)
    ffn_up = P.dram("ffn_up", [nlayers, D, 2 * FFN], F32, EI)
    ffn_down = P.dram("ffn_down", [nlayers, FFN, D], F32, EI)
    outT = P.dram("outT", [D, S], F32, "ExternalOutput")
    dbg_out = {}

    x1_d = P.dram("x1_d", [D, S], F32)
    x2_d = P.dram("x2_d", [D, S], F32)
    zs_d = P.dram("zs_d", [512, S], F32)
    xbc_d = P.dram("xbc_d", [1024, S], F32)
    dt_d = P.dram("dt_d", [8, S], F32)
    acum_d = P.dram("acum_d", [8, S], F32)
    qk_d = P.dram("qk_d", [4, 128, S], BF16)
    v_d = P.dram("v_d", [S, 256], BF16)
    mix_d = P.dram("mix_d", [D, S], BF16)
    act_d = P.dram("act_d", [FFN, S], BF16)

    consts = P.sb("consts", [128, NCONST], F32)
    vecs = P.sb("vecs", [128, DEPTH, NV], F32)
    modv = P.sb("modv", [128, DEPTH, 48], F32)
    dervec = P.sb("dervec", [128, DEPTH, 32], F32)
    epsv = P.sb("epsv", [128, 1], F32)
    ident_bf = P.sb("ident_bf", [128, 128], BF16)
    ones_bf = P.sb("ones_bf", [128, 128], BF16)
    amask_bf = P.sb("amask_bf", [128, 512], BF16)
    ropeC = P.sb("ropeC", [128, S], F32)
    ropeS = P.sb("ropeS", [128, S], F32)
    slab = [P.sb("slab%d" % i, [128, 8, 128], BF16) for i in range(4)]
    pbd_bf = P.sb("pbd_bf", [128, 2, 128], BF16)
    PS = [P.ps("ps%d" % i, [128, 512], F32) for i in range(8)]
    AR = Arena(P, 38400)
    P.eps_ap = epsv[:, 0:1]

    ident = consts[:, K_ID:K_ID + 128]
    ones32 = consts[:, K_ONES:K_ONES + 128]

    P.dma(consts[:, :], consts_in[:, :])
    P.dma(vecs[:, 0:nlayers, :], vecs_in[:, :, :].map(lambda a: a.rearrange("l p n -> p l n")))
    P.memset("dve", epsv[:, :], EPS)
    P.copy("dve", ident_bf[:, :], ident)
    P.copy("dve", ones_bf[:, :], ones32)
    P.copy("dve", amask_bf[:, 0:256], consts[:, K_AM:K_AM + 256])
    P.copy("dve", amask_bf[:, 256:512], consts[:, K_AM:K_AM + 256])

    AR.reset()
    cact = AR.alloc([128, 8])
    P.act(cact[:, :], consts[:, K_C:K_C + 8], AF.Silu)
    adab = [AR.alloc([128, 8, 512]) for _ in range(2)]
    psm = PS[0]
    for l in range(nlayers):
        for pc in range(12):
            buf = adab[pc % 2]
            P.dma(buf[:, :, :], ada_w[l, :, pc * 512:(pc + 1) * 512].map(
                lambda a: a.rearrange("(k p) n -> p k n", p=128)))
            for j in range(4):
                col = pc * 4 + j
                for k in range(8):
                    P.mm(psm[:, col:col + 1], buf[:, k, j * 128:(j + 1) * 128], cact[:, k:k + 1],
                         start=(k == 0), stop=(k == 7), sig=(k == 7))
        P.tt("dve", modv[:, l, :], psm[:, 0:48], vecs[:, l, V_ADAB:V_ADAB + 48], ALU.add)
        P.ts("dve", dervec[:, l, 0:8], modv[:, l, 8:16], 1.0, ALU.add)
        P.tt("dve", dervec[:, l, 0:8], dervec[:, l, 0:8], vecs[:, l, V_N1:V_N1 + 8], ALU.mult)
        P.ts("dve", dervec[:, l, 8:16], modv[:, l, 32:40], 1.0, ALU.add)
        P.tt("dve", dervec[:, l, 8:16], dervec[:, l, 8:16], vecs[:, l, V_N2:V_N2 + 8], ALU.mult)
        P.act(dervec[:, l, 16:17], vecs[:, l, V_ALOG:V_ALOG + 1], AF.Exp)
        P.ts("dve", dervec[:, l, 16:17], dervec[:, l, 16:17], -1.0, ALU.mult)

    TWO_PI = 2.0 * np.pi
    posi = AR.alloc([128, S], I32)
    ang = AR.alloc([128, S])
    tmp = AR.alloc([128, S])
    kf = AR.alloc([128, S])
    P.dma(posi[:, :], pos_in[:, :].map(lambda a: a.broadcast_to([128, S])))
    P.copy("dve", ang[:, :], posi[:, :])
    P.ts("dve", ang[:, :], ang[:, :], consts[:, K_IFR:K_IFR + 1], ALU.mult)
    ki = Tile(posi.t.bitcast(I32)) if False else posi

    def wrap(r):
        P.ts("dve", tmp[:, :], r[:, :], float(np.pi), ALU.is_gt, TWO_PI, ALU.mult)
        P.tt("dve", r[:, :], r[:, :], tmp[:, :], ALU.subtract)
        P.ts("dve", tmp[:, :], r[:, :], float(-np.pi), ALU.is_lt, TWO_PI, ALU.mult)
        P.tt("dve", r[:, :], r[:, :], tmp[:, :], ALU.add)
        P.ts("dve", r[:, :], r[:, :], 3.14159, ALU.min, -3.14159, ALU.max)

    P.ts("dve", tmp[:, :], ang[:, :], float(1.0 / TWO_PI), ALU.mult)
    P.copy("dve", ki[:, :], tmp[:, :])
    P.copy("dve", kf[:, :], ki[:, :])
    P.stt(ang[:, :], kf[:, :], -TWO_PI, ang[:, :], ALU.mult, ALU.add)
    wrap(ang)
    P.act(ropeS[:, :], ang[:, :], AF.Sin, scale=consts[:, K_SGN:K_SGN + 1])
    P.ts("dve", ang[:, :], ang[:, :], float(np.pi / 2), ALU.add)
    wrap(ang)
    P.act(ropeC[:, :], ang[:, :], AF.Sin)
    _barrier(P)

    C = dict(locals())
    xin = xT_in
    for l in range(nlayers):
        xin = layer(P, l, C, xin, last=(l == nlayers - 1), stop_after=stop_after)
    P.final_tiles = [outT] + [C[n] for n in dbg]
    return P


HALO = 16


def phase_norm(P, C, l, xsrc, Aview, Bview, h_full):
    AR = C["AR"]
    PS = C["PS"]
    ones32 = C["ones32"]
    xt = [AR.alloc([128, 8, TT]) for _ in range(2)]
    sq = [AR.alloc([128, 8, TT]) for _ in range(2)]
    rs = [AR.alloc([128, TT]) for _ in range(2)]
    for t in range(NT):
        b = t % 2
        P.dma(xt[b][:, :, :], xsrc[:, t * TT:(t + 1) * TT].map(lambda a: a.rearrange("(k p) n -> p k n", p=128)))
        P.act(sq[b][:, :, :], xt[b][:, :, :], AF.Square)
        rmsnorm_rstd(P, PS[7][:, :], [sq[b][:, k, :] for k in range(8)], ones32, rs[b][:, :], D)
        P.tt("dve", xt[b][:, :, :], xt[b][:, :, :],
             rs[b][:, :].map(lambda a: a.unsqueeze(1).to_broadcast([128, 8, TT])), ALU.mult)
        for k in range(8):
            P.act(h_full[:, k, t * TT:(t + 1) * TT], xt[b][:, k, :], AF.Identity,
                  bias=Bview[k], scale=Aview[k])


def layer(P, l, C, xin, last, stop_after=None):
    AR = C["AR"]
    PS = C["PS"]
    consts, vecs, modv, dervec = C["consts"], C["vecs"], C["modv"], C["dervec"]
    slab = C["slab"]
    w_in = C["w_in"]
    AR.reset()
    h_full = AR.alloc([128, 8, S], BF16)
    mark = AR.off
    phase_norm(P, C, l, xin, [dervec[:, l, k:k + 1] for k in range(8)],
               [modv[:, l, k:k + 1] for k in range(8)], h_full)
    _barrier(P)
    AR.off = mark
    if stop_after == "A0":
        hd = C["mix_d"]
        for k in range(8):
            P.dma(hd[k * 128:(k + 1) * 128, :], h_full[:, k, :])
        return xin

    PB = [AR.alloc([128, HALO + S]) for _ in range(2)]
    OB = [AR.alloc([128, HALO + S]) for _ in range(2)]
    stg = [AR.alloc([128, S], BF16) for _ in range(2)]
    wv = AR.alloc([128, 8, 256], BF16)
    for b_ in PB + OB:
        P.memset("pool", b_[:, 0:HALO], 0.0)
    st = {"slab": 0, "ps": 0, "pb": 0, "ob": 0, "stg": 0}

    def nxt(key, n):
        v = st[key]
        st[key] = (v + 1) % n
        return v

    def load_slab(c0, n):
        sl = slab[nxt("slab", 4)]
        P.dma(sl[:, :, 0:n], w_in[l, :, c0:c0 + n].map(lambda a: a.rearrange("(k p) n -> p k n", p=128)), e="pool")
        return sl

    def project(sl, n, evac):
        for t in range(NT):
            ps = PS[nxt("ps", 6)]
            for k in range(8):
                P.mm(ps[0:n, :], sl[:, k, 0:n], h_full[:, k, t * TT:(t + 1) * TT],
                     start=(k == 0), stop=(k == 7), sig=(k == 7))
            evac(t, ps)

    def tsl(t):
        return slice(HALO + t * TT, HALO + (t + 1) * TT)

    sl = load_slab(C_DT, 8)
    pb = PB[nxt("pb", 2)]
    project(sl, 8, lambda t, ps: P.copy("act", pb[0:8, tsl(t)], ps[0:8, :]))
    ob = OB[nxt("ob", 2)]
    ob2 = OB[nxt("ob", 2)]
    xv = pb[0:8, HALO:HALO + S]
    t1 = ob[0:8, HALO:HALO + S]
    t2 = ob2[0:8, HALO:HALO + S]
    dtb = vecs[0:8, l, V_DTB:V_DTB + 1]
    P.ts("dve", xv, xv, dtb, ALU.add)
    P.ts("dve", t1, xv, -1.0, ALU.mult)
    P.tt("dve", t1, t1, xv, ALU.max)
    P.act(t1, t1, AF.Exp, scale=-1.0)
    P.act(t1, t1, AF.Ln, bias=1.0)
    P.ts("dve", xv, xv, 0.0, ALU.max)
    P.tt("dve", xv, xv, t1, ALU.add)
    P.dma(C["dt_d"][:, :], xv)
    P.ts("dve", t1, xv, dervec[0:8, l, 16:17], ALU.mult)
    P.memset("pool", t2, 1.0)
    P.memset("pool", ob2[0:8, HALO:HALO + S:256], 0.0)
    pb2 = PB[nxt("pb", 2)]
    ac = pb2[0:8, HALO:HALO + S]
    t1a, t2a, aca = t1.ap, t2.ap, ac.ap
    P.emit("dve", lambda v: v.tensor_tensor_scan(aca, t2a, t1a, 0.0, ALU.mult, ALU.add), [t1, t2], [ac])
    P.dma(C["acum_d"][:, :], ac)

    for c in range(4):
        sl = load_slab(C_Z + c * 128, 128)
        ob = OB[nxt("ob", 2)]
        project(sl, 128, lambda t, ps, ob=ob: P.act(ob[:, tsl(t)], ps[:, :], AF.Silu))
        P.dma(C["zs_d"][c * 128:(c + 1) * 128, :], ob[:, HALO:HALO + S])

    pend_xbc = []
    for c in range(8):
        sl = load_slab(C_XBC + c * 128, 128)
        pb = PB[nxt("pb", 2)]
        project(sl, 128, lambda t, ps, pb=pb: P.copy("act", pb[:, tsl(t)], ps[:, :]))
        ob = OB[nxt("ob", 2)]
        o = ob[:, HALO:HALO + S]
        w = lambda tap: vecs[:, l, V_SCW + c * 4 + tap:V_SCW + c * 4 + tap + 1]
        P.act(o, pb[:, HALO:HALO + S], AF.Identity, bias=vecs[:, l, V_SCB + c:V_SCB + c + 1], scale=w(3))
        for tap in (2, 1, 0):
            sh = 3 - tap
            P.stt(o, pb[:, HALO - sh:HALO - sh + S], w(tap), o, ALU.mult, ALU.add)
        if pend_xbc:
            pend_xbc.pop()()

        def _fin(o=o, c=c):
            P.act(o, o, AF.Silu)
            P.dma(C["xbc_d"][c * 128:(c + 1) * 128, :], o)
        pend_xbc.append(_fin)
    if pend_xbc:
        pend_xbc.pop()()

    for c in range(2):
        sl = load_slab(C_U + c * 128, 128)
        pb = PB[nxt("pb", 2)]
        project(sl, 128, lambda t, ps, pb=pb: P.copy("act", pb[:, tsl(t)], ps[:, :]))
        X1 = OB[0]
        X2 = OB[1]
        X3 = PB[nxt("pb", 2)]

        def shadd(e, dst, src, sh, lo=0, hi=128):
            P.tt(e, dst[lo:hi, HALO:HALO + S], src[lo:hi, HALO:HALO + S], src[lo:hi, HALO - sh:HALO - sh + S], ALU.add)

        shadd("dve", X1, pb, 1)
        if c == 0:
            shadd("dve", X2, X1, 2, 64, 128)
            P.copy("pool", X2[0:64, HALO:HALO + S], X1[0:64, HALO:HALO + S])
            W = X2
        else:
            shadd("dve", X2, X1, 2)
            shadd("dve", X3, X2, 4)
            shadd("dve", X1, X3, 8, 64, 128)
            P.copy("pool", X1[0:64, HALO:HALO + S], X3[0:64, HALO:HALO + S])
            W = X1
        P.tt("dve", W[:, HALO:HALO + 16], W[:, HALO:HALO + 16], consts[:, K_INVC + c * 16:K_INVC + (c + 1) * 16], ALU.mult)
        P.ts("dve", W[:, HALO + 16:HALO + S], W[:, HALO + 16:HALO + S], consts[:, K_INVW + c:K_INVW + c + 1], ALU.mult)
        sg = stg[nxt("stg", 2)]
        P.tt("dve", sg[:, :], W[:, HALO:HALO + S], pb[:, HALO:HALO + S], ALU.subtract)
        if c == 0:
            pbt = AR.alloc([128, 2, 128])
            P.dma(pbt[:, :, :], C["pool_bd"][l, :, :, :])
            P.copy("dve", C["pbd_bf"][:, :, :], pbt[:, :, :])
        so = stg[nxt("stg", 2)]
        for t in range(NT):
            ps = PS[nxt("ps", 6)]
            P.mm(ps[:, :], C["pbd_bf"][:, c, :], sg[:, t * TT:(t + 1) * TT])
            P.act(so[:, t * TT:(t + 1) * TT], ps[:, :], AF.Identity, scale=vecs[:, l, V_PSC + c:V_PSC + c + 1])
        P.dma(C["mix_d"][512 + c * 128:512 + (c + 1) * 128, :], so[:, :])

    for qi, (c0, cp0) in enumerate([(C_Q, IN_W), (C_Q + 128, IN_W + 128), (C_K, IN_W + 256), (C_K + 128, IN_W + 384)]):
        sl = load_slab(c0, 128)
        pa = PB[nxt("pb", 2)]
        project(sl, 128, lambda t, ps, pa=pa: P.copy("act", pa[:, tsl(t)], ps[:, :]))
        sl = load_slab(cp0, 128)
        pp = PB[nxt("pb", 2)]
        project(sl, 128, lambda t, ps, pp=pp: P.copy("act", pp[:, tsl(t)], ps[:, :]))
        ob = OB[nxt("ob", 2)]
        o = ob[:, HALO:HALO + S]
        P.tt("dve", o, pa[:, HALO:HALO + S], C["ropeC"][:, :], ALU.mult)
        P.tt("pool", pp[:, HALO:HALO + S], pp[:, HALO:HALO + S], C["ropeS"][:, :], ALU.mult)
        sg = stg[nxt("stg", 2)]
        P.tt("dve", sg[:, :], o, pp[:, HALO:HALO + S], ALU.add)
        P.dma(C["qk_d"][qi, :, :], sg[:, :])

    P.dma(wv[:, :, :], w_in[l, :, C_V:C_V + 256].map(lambda a: a.rearrange("(k p) n -> p k n", p=128)), e="pool")
    for tb2 in range(16):
        ps = PS[nxt("ps", 6)]
        for h in range(2):
            tb = tb2 * 2 + h
            for k in range(8):
                P.mm(ps[:, h * 256:(h + 1) * 256], h_full[:, k, tb * 128:(tb + 1) * 128], wv[:, k, :],
                     start=(k == 0), stop=(k == 7), sig=(k == 7))
        sg = stg[nxt("stg", 2)]
        P.copy("act", sg[:, 0:512], ps[:, :])
        P.dma(C["v_d"][tb2 * 256:(tb2 + 1) * 256, :].map(lambda a: a.rearrange("(b p) c -> p b c", p=128)),
              sg[:, 0:512].map(lambda a: a.rearrange("p (b c) -> p b c", b=2)))
    _barrier(P)
    if stop_after == "A1":
        return xin
    phase_ssd(P, C, l)
    if stop_after == "A2":
        return xin
    phase_attn(P, C, l)
    if stop_after == "B":
        return xin
    return phase_ffn(P, C, l, xin, last)


def make_in_maps(inp, cores, nl=DEPTH):
    shared = {
        "vecs": np.stack([pack_vecs(inp, l) for l in range(DEPTH)]),
        "ada_w": np.ascontiguousarray(inp["ada_w"], dtype=np.float32),
        "w_in": np.stack([pack_w_in(np.asarray(inp["w_in"][l], np.float32)) for l in range(DEPTH)]),
        "pool_bd": np.stack([pack_pool(np.asarray(inp["pool_w"][l], np.float32)) for l in range(DEPTH)]),
        "w_out": np.ascontiguousarray(inp["w_out"], dtype=np.float32),
        "ffn_up": np.ascontiguousarray(inp["ffn_up"], dtype=np.float32),
        "ffn_down": np.ascontiguousarray(inp["ffn_down"], dtype=np.float32),
    }
    shared = {k: np.ascontiguousarray(v[0:nl]) for k, v in shared.items()}
    maps = []
    for b in cores:
        m = dict(shared)
        m["xT"] = np.ascontiguousarray(np.asarray(inp["x"][b], np.float32).T)
        m["pos"] = np.ascontiguousarray(np.asarray(inp["positions"][b], np.int32).reshape(1, S))
        m["consts"] = pack_consts(inp, b)
        maps.append(m)
    return maps


_CACHE = {}


def kernel(**inputs):
    inp = {k: np.asarray(v) for k, v in inputs.items()}
    if "prog" not in _CACHE:
        _CACHE["prog"] = build().finish_prog()
    nc = _CACHE["prog"]
    maps = make_in_maps(inp, list(range(8)))
    res = run_bass_kernel_spmd(nc, maps, core_ids=list(range(8)))
    out = np.stack([np.ascontiguousarray(res.results[b]["outT"].T) for b in range(8)])
    return out.astype(np.float32)


def phase_ssd(P, C, l):
    AR, PS = C["AR"], C["PS"]
    consts, vecs, dervec = C["consts"], C["vecs"], C["dervec"]
    ident, ones32 = C["ident"], C["ones32"]
    AR.reset()
    xb = [AR.alloc([128, 8, 256]) for _ in range(2)]
    zb = [AR.alloc([128, 4, 256]) for _ in range(2)]
    acb = [AR.alloc([128, 8, 256]) for _ in range(2)]
    dA = [AR.alloc([128, 2, 256]) for _ in range(2)]
    ebc = AR.alloc([128, 8, 256])
    tk = AR.alloc([128, 2, 2, 8])
    dec = AR.alloc([128, 2, 8])
    dd = AR.alloc([128, 2, 8])
    bcT = AR.alloc([128, 4, 256], BF16)
    cms = AR.alloc([128, 8, 256], BF16)
    xdt = AR.alloc([128, 2, 512], BF16)
    xdec = AR.alloc([128, 2, 512], BF16)
    bmtok = AR.alloc([128, 2, 256], BF16)
    seg = [AR.alloc([128, 384]) for _ in range(2)]
    LT = [AR.alloc([128, 384]) for _ in range(2)]
    MT = [AR.alloc([128, 384], BF16) for _ in range(3)]
    hin = AR.alloc([128, 512])
    hin_bf = [AR.alloc([128, 512], BF16) for _ in range(2)]
    ybuf = AR.alloc([128, 4, 256])
    ysq = AR.alloc([128, 4, 256])
    rs = AR.alloc([128, 2, 256])
    obf = [AR.alloc([128, 4, 256], BF16) for _ in range(2)]
    negm = consts[:, K_NEG:K_NEG + 384]
    P.memset("pool", hin[:, :], 0.0)
    P.memset("pool", hin_bf[0][:, :], 0.0)
    XS0, XS1, BMT, CB0, CB1, Y01, Y23, ST = PS
    for c in range(16):
        b = c % 2
        tok = slice(c * 256, (c + 1) * 256)
        P.dma(xb[b][:, :, :], C["xbc_d"][:, tok].map(lambda a: a.rearrange("(k p) n -> p k n", p=128)))
        P.dma(zb[b][:, :, :], C["zs_d"][:, tok].map(lambda a: a.rearrange("(k p) n -> p k n", p=128)))
        P.dma(acb[b][:, :, :], C["acum_d"][:, tok].map(lambda a: a.unsqueeze(0).broadcast_to([128, 8, 256])))
        P.dma(dA[b][0:8, 0, :], C["dt_d"][:, tok])
        P.dma(dA[b][0:8, 1, :], C["acum_d"][:, tok])
        x_, a_ = xb[b], acb[b]
        for sb in range(2):
            for wh in range(2):
                o = (sb * 2 + wh) * 8
                P.mm(CB1[:, o:o + 8], dA[b][0:8, wh, sb * 128:(sb + 1) * 128], ident.map(lambda a: a[0:8, 0:8]))
        P.copy("dve", tk[:, :, :, :], CB1[:, 0:32].map(lambda a: a.rearrange("p (s w h) -> p s w h", s=2, w=2)))
        P.tt("dve", dec[:, :, :], a_[:, :, 255].map(lambda a: a.unsqueeze(1).to_broadcast([128, 2, 8])),
             tk[:, :, 1, :], ALU.subtract)
        P.act(dec[:, :, :], dec[:, :, :], AF.Exp)
        P.tt("dve", dd[:, :, :], dec[:, :, :], tk[:, :, 0, :], ALU.mult)
        P.act(ebc[:, :, :], a_[:, :, :], AF.Exp)
        P.copy("pool", bcT[:, :, :], x_[:, 4:8, :])
        for g in range(2):
            P.tt("pool", cms[:, 4 * g:4 * g + 4, :],
                 x_[:, 6 + g, :].map(lambda a: a.unsqueeze(1).to_broadcast([128, 4, 256])),
                 ebc[:, 4 * g:4 * g + 4, :], ALU.mult)
        for sb, XS in enumerate((XS0, XS1)):
            for ch in range(4):
                P.transpose(XS[:, ch * 128:(ch + 1) * 128], x_[:, ch, sb * 128:(sb + 1) * 128], ident, sig=(ch == 3))
            xs3 = XS[:, :].map(lambda a: a.rearrange("p (h e) -> p h e", h=8))
            P.tt("dve", xdt[:, sb, :].map(lambda a: a.rearrange("p (h e) -> p h e", h=8)), xs3,
                 tk[:, sb, 0, :].map(lambda a: a.unsqueeze(2).to_broadcast([128, 8, 64])), ALU.mult)
            P.tt("dve", xdec[:, sb, :].map(lambda a: a.rearrange("p (h e) -> p h e", h=8)), xs3,
                 dd[:, sb, :].map(lambda a: a.unsqueeze(2).to_broadcast([128, 8, 64])), ALU.mult)
            for g in range(2):
                P.transpose(BMT[:, sb * 256 + g * 128:sb * 256 + (g + 1) * 128], x_[:, 4 + g, sb * 128:(sb + 1) * 128],
                            ident, sig=(g == 1))
        P.copy("act", bmtok[:, :, :], BMT[:, :].map(lambda a: a.rearrange("p (s n) -> p s n", s=2)))
        hb = hin_bf[c % 2]
        for g in range(2):
            CB = CB0 if g == 0 else CB1
            P.mm(CB[:, 0:256], bcT[:, g, 0:128], bcT[:, 2 + g, 0:256])
            P.mm(CB[:, 256:384], bcT[:, g, 128:256], bcT[:, 2 + g, 128:256])
            for j in range(4):
                h = 4 * g + j
                sg_, lt_, mt_ = seg[h % 2], LT[h % 2], MT[h % 3]
                P.stt(sg_[:, 0:256], a_[:, h, 0:256], tk[:, 0, 1, h:h + 1], negm.map(lambda a: a[:, 0:256]),
                      ALU.subtract, ALU.add)
                P.stt(sg_[:, 256:384], a_[:, h, 128:256], tk[:, 1, 1, h:h + 1], negm.map(lambda a: a[:, 256:384]),
                      ALU.subtract, ALU.add)
                P.act(lt_[:, :], sg_[:, :], AF.Exp)
                P.tt("dve", mt_[:, :], CB[:, 0:384], lt_[:, :], ALU.mult)
                Y = Y01 if h < 4 else Y23
                col = ((h // 2) % 2) * 256
                r0 = (h % 2) * 64
                yv = lambda a, b_: Y[r0:r0 + 64, col + a:col + b_]
                P.mm(yv(0, 256), xdt[:, 0, h * 64:(h + 1) * 64], mt_[:, 0:256], start=True, stop=False, sig=False)
                P.mm(yv(128, 256), xdt[:, 1, h * 64:(h + 1) * 64], mt_[:, 256:384], start=False, stop=(c == 0),
                     sig=(c == 0))
                if c > 0:
                    P.mm(yv(0, 256), hb[:, h * 64:(h + 1) * 64], cms[:, h, :], start=False, stop=True)
                for sb in range(2):
                    P.mm(ST[:, h * 64:(h + 1) * 64], bmtok[:, sb, g * 128:(g + 1) * 128],
                         xdec[:, sb, h * 64:(h + 1) * 64], start=(sb == 0), stop=(sb == 1), sig=(sb == 1))
        h3 = lambda t_: t_[:, :].map(lambda a: a.rearrange("p (h e) -> p h e", h=8))
        P.tt("dve", h3(hin), h3(hin), ebc[:, :, 255].map(lambda a: a.unsqueeze(2).to_broadcast([128, 8, 64])), ALU.mult)
        P.tt("dve", hin[:, :], hin[:, :], ST[:, :], ALU.add)
        P.copy("pool", hin_bf[(c + 1) % 2][:, :], hin[:, :])
        for kk in range(4):
            Y = Y01 if kk < 2 else Y23
            P.stt(ybuf[:, kk, :], x_[:, kk, :], vecs[:, l, V_SD + kk:V_SD + kk + 1],
                  Y[:, (kk % 2) * 256:(kk % 2 + 1) * 256], ALU.mult, ALU.add)
        P.tt("dve", ybuf[:, :, :], ybuf[:, :, :], zb[b][:, :, :], ALU.mult)
        P.act(ysq[:, :, :], ybuf[:, :, :], AF.Square)
        for g in range(2):
            rmsnorm_rstd(P, XS0[:, g * 256:(g + 1) * 256], [ysq[:, 2 * g, :], ysq[:, 2 * g + 1, :]], ones32,
                         rs[:, g, :], 256)
        ob = obf[b]
        for kk in range(4):
            P.stt(ob[:, kk, :], ybuf[:, kk, :], vecs[:, l, V_SNG + kk:V_SNG + kk + 1], rs[:, kk // 2, :],
                  ALU.mult, ALU.mult)
        P.dma(C["mix_d"][0:512, tok].map(lambda a: a.rearrange("(k p) n -> p k n", p=128)), ob[:, :, :])
    _barrier(P)


def phase_attn(P, C, l):
    AR, PS = C["AR"], C["PS"]
    AR.reset()
    qk = [AR.alloc([128, S], BF16) for _ in range(4)]
    acc = AR.alloc([128, 2, S])
    NB = 4
    vraw = [AR.alloc([128, 256], BF16) for _ in range(NB)]
    V1 = [AR.alloc([128, 4, 128], BF16) for _ in range(NB)]
    PT = [AR.alloc([128, 2, 256], BF16) for _ in range(3)]
    onesA = AR.alloc([128, 128], BF16)
    onesB = AR.alloc([128, 128], BF16)
    rec = AR.alloc([128, S])
    osb = AR.alloc([128, S], BF16)
    amask = C["amask_bf"][:, :].map(lambda a: a.rearrange("p (h q) -> p h q", h=2))
    for i in range(4):
        P.dma(qk[i][:, :], C["qk_d"][i, :, :])
    for v in V1:
        P.memset("pool", v[:, :, :], 0.0)
    P.memset("pool", onesA[:, :], 0.0)
    P.memset("pool", onesB[:, :], 0.0)
    P.memset("pool", onesA[:, 0:64], 1.0)
    P.memset("pool", onesB[:, 64:128], 1.0)
    it = 0
    qpad = [AR.alloc([128, S], BF16) for _ in range(2)]
    for pair in range(2):
        qT, kT = qk[pair], qk[2 + pair]
        P.memset("pool", acc[:, :, :], 0.0)
        P.memset("pool", qpad[0][64:128, :], 0.0)
        P.memset("pool", qpad[1][0:64, :], 0.0)
        P.copy("pool", qpad[0][0:64, :], qT[0:64, :])
        P.copy("pool", qpad[1][64:128, :], qT[64:128, :])
        for d in (1, 4, 16):
            nb = S // d // 128
            for r in range(d):
                for j in range(nb):
                    nq = 256 if j + 1 < nb else 128
                    k0 = r + d * 128 * j
                    ksl = slice(k0, k0 + d * 127 + 1, d)
                    qsl = slice(k0, k0 + d * (nq - 1) + 1, d)
                    vb = it % NB
                    P.dma(vraw[vb][:, :], C["v_d"][ksl, :])
                    v4 = vraw[vb][:, :].map(lambda a: a.rearrange("p (h e) -> p h e", h=4))
                    P.copy("pool", V1[vb][:, 0:4:2, 0:64], v4.map(lambda a: a[:, 0:4:2, :]))
                    P.copy("pool", V1[vb][:, 1:4:2, 64:128], v4.map(lambda a: a[:, 1:4:2, :]))
                    sp = PS[it % 4]
                    for hh in range(2):
                        P.mm(sp[:, hh * 256:hh * 256 + nq], kT[:, ksl], qpad[hh][:, qsl], sig=(hh == 1))
                    pt = PT[it % 3]
                    sp3 = sp[:, :].map(lambda a: a.rearrange("p (h q) -> p h q", h=2))
                    P.act(pt[:, :, 0:nq], sp3.map(lambda a: a[:, :, 0:nq]), AF.Exp, scale=0.125)
                    P.tt("dve", pt[:, :, 0:nq], pt[:, :, 0:nq], amask.map(lambda a: a[:, :, 0:nq]), ALU.mult)
                    op = PS[4 + it % 4]
                    hA, hB = 2 * pair, 2 * pair + 1
                    P.mm(op[:, 0:nq], V1[vb][:, hA, :], pt[:, 0, 0:nq], start=True, stop=False, sig=False)
                    P.mm(op[:, 0:nq], V1[vb][:, hB, :], pt[:, 1, 0:nq], start=False, stop=True, sig=False)
                    P.mm(op[:, 256:256 + nq], onesA[:, :], pt[:, 0, 0:nq], start=True, stop=False, sig=False)
                    P.mm(op[:, 256:256 + nq], onesB[:, :], pt[:, 1, 0:nq], start=False, stop=True, sig=True)
                    op3 = op[:, :].map(lambda a: a.rearrange("p (h q) -> p h q", h=2))
                    av = acc[:, :, qsl]
                    P.tt("dve", av, av, op3.map(lambda a: a[:, :, 0:nq]), ALU.add)
                    it += 1
        rca, aa = rec[:, :].ap, acc[:, 1, :].ap
        P.emit("dve", lambda v: v.reciprocal(rca, aa), [acc], [rec])
        P.tt("dve", osb[:, :], acc[:, 0, :], rec[:, :], ALU.mult)
        P.dma(C["mix_d"][768 + pair * 128:768 + (pair + 1) * 128, :], osb[:, :])
    _barrier(P)


def phase_ffn(P, C, l, xin, last):
    AR, PS = C["AR"], C["PS"]
    consts, vecs, modv, dervec, slab = C["consts"], C["vecs"], C["modv"], C["dervec"], C["slab"]
    st = {"slab": 0, "ps": 0}

    def nxt(key, n):
        v = st[key]
        st[key] = (v + 1) % n
        return v

    AR.reset()
    mixf = AR.alloc([128, 8, S], BF16)
    xrow = [AR.alloc([128, S]) for _ in range(2)]
    for k in range(8):
        P.dma(mixf[:, k, :], C["mix_d"][k * 128:(k + 1) * 128, :])
    for n in range(8):
        sl = slab[nxt("slab", 4)]
        P.dma(sl[:, :, :], C["w_out"][l, :, n * 128:(n + 1) * 128].map(lambda a: a.rearrange("(k p) n -> p k n", p=128)),
              e="pool")
        xr = xrow[n % 2]
        P.dma(xr[:, :], xin[n * 128:(n + 1) * 128, :])
        for t in range(NT):
            ps = PS[nxt("ps", 7)]
            for k in range(8):
                P.mm(ps[:, :], sl[:, k, :], mixf[:, k, t * TT:(t + 1) * TT], start=(k == 0), stop=(k == 7), sig=(k == 7))
            P.stt(xr[:, t * TT:(t + 1) * TT], ps[:, :], modv[:, l, 16 + n:17 + n], xr[:, t * TT:(t + 1) * TT],
                  ALU.mult, ALU.add)
        P.dma(C["x1_d"][n * 128:(n + 1) * 128, :], xr[:, :])
    _barrier(P)
    AR.reset()
    h2 = AR.alloc([128, 8, S], BF16)
    mark = AR.off
    phase_norm(P, C, l, C["x1_d"], [dervec[:, l, 8 + k:9 + k] for k in range(8)],
               [modv[:, l, 24 + k:25 + k] for k in range(8)], h2)
    _barrier(P)
    AR.off = mark
    XG = AR.alloc([128, HALO + S])
    XU = AR.alloc([128, HALO + S])
    YG = AR.alloc([128, S])
    YU = AR.alloc([128, S])
    actb = [AR.alloc([128, S], BF16) for _ in range(2)]
    P.memset("pool", XG[:, 0:HALO], 0.0)
    P.memset("pool", XU[:, 0:HALO], 0.0)
    for i in range(NFC):
        for (X, Yb, cc, col0, ev) in ((XG, YG, i, i * 128, "act"), (XU, YU, NFC + i, FFN + i * 128, "act")):
            sl = slab[nxt("slab", 4)]
            P.dma(sl[:, :, :], C["ffn_up"][l, :, col0:col0 + 128].map(lambda a: a.rearrange("(k p) n -> p k n", p=128)),
                  e="pool")
            for t in range(NT):
                ps = PS[nxt("ps", 7)]
                for k in range(8):
                    P.mm(ps[:, :], sl[:, k, :], h2[:, k, t * TT:(t + 1) * TT], start=(k == 0), stop=(k == 7),
                         sig=(k == 7))
                P.copy(ev, X[:, HALO + t * TT:HALO + (t + 1) * TT], ps[:, :])
            w = lambda tap, cc=cc: vecs[:, l, V_FCW + cc * 3 + tap:V_FCW + cc * 3 + tap + 1]
            P.act(Yb[:, :], X[:, HALO:HALO + S], AF.Identity, bias=vecs[:, l, V_FCB + cc:V_FCB + cc + 1], scale=w(2))
            P.stt(Yb[:, :], X[:, HALO - 1:HALO - 1 + S], w(1), Yb[:, :], ALU.mult, ALU.add)
            P.stt(Yb[:, :], X[:, HALO - 2:HALO - 2 + S], w(0), Yb[:, :], ALU.mult, ALU.add)
        P.act(YG[:, :], YG[:, :], AF.Silu)
        ab = actb[i % 2]
        P.tt("pool", ab[:, :], YG[:, :], YU[:, :], ALU.mult)
        P.dma(C["act_d"][i * 128:(i + 1) * 128, :], ab[:, :])
    _barrier(P)
    AR.reset()
    wd = AR.alloc([128, NFC, D], BF16)
    at = [AR.alloc([128, NFC, TT], BF16) for _ in range(2)]
    xt = [AR.alloc([128, 8, TT]) for _ in range(2)]
    sq = AR.alloc([128, 8, TT])
    rsf = AR.alloc([128, TT])
    for kq in range(0, NFC, 2):
        P.dma(wd[:, kq:kq + 2, :], C["ffn_down"][l, kq * 128:(kq + 2) * 128, :].map(
            lambda a: a.rearrange("(k p) n -> p k n", p=128)), e="pool")
    xout = C["x2_d"]
    for t in range(NT):
        b = t % 2
        tsl_ = slice(t * TT, (t + 1) * TT)
        P.dma(at[b][:, 0:11, :], C["act_d"][0:11 * 128, tsl_].map(lambda a: a.rearrange("(k p) n -> p k n", p=128)))
        P.dma(at[b][:, 11:22, :], C["act_d"][11 * 128:22 * 128, tsl_].map(lambda a: a.rearrange("(k p) n -> p k n", p=128)))
        P.dma(xt[b][:, :, :], C["x1_d"][:, tsl_].map(lambda a: a.rearrange("(k p) n -> p k n", p=128)))
        for n in range(8):
            ps = PS[nxt("ps", 7)]
            for k in range(NFC):
                P.mm(ps[:, :], wd[:, k, n * 128:(n + 1) * 128], at[b][:, k, :], start=(k == 0), stop=(k == NFC - 1),
                     sig=(k == NFC - 1))
            P.stt(xt[b][:, n, :], ps[:, :], modv[:, l, 40 + n:41 + n], xt[b][:, n, :], ALU.mult, ALU.add)
        if not last:
            P.dma(xout[:, tsl_].map(lambda a: a.rearrange("(k p) n -> p k n", p=128)), xt[b][:, :, :])
        else:
            P.act(sq[:, :, :], xt[b][:, :, :], AF.Square)
            rmsnorm_rstd(P, PS[7][:, :], [sq[:, k, :] for k in range(8)], C["ones32"], rsf[:, :], D)
            P.tt("dve", xt[b][:, :, :], xt[b][:, :, :],
                 rsf[:, :].map(lambda a: a.unsqueeze(1).to_broadcast([128, 8, TT])), ALU.mult)
            P.tt("dve", xt[b][:, :, :], xt[b][:, :, :],
                 consts[:, K_FG:K_FG + 8].map(lambda a: a.unsqueeze(2).to_broadcast([128, 8, TT])), ALU.mult)
            P.dma(C["outT"][:, tsl_].map(lambda a: a.rearrange("(k p) n -> p k n", p=128)), xt[b][:, :, :])
    _barrier(P)
    return xout
```
